# Optimizing a Trainium2 kernel written in Bass

```python
import math
import jax, jax.numpy as jnp
from jax import lax
import numpy as np

D_MODEL = 2048
BATCH = 1
SEQ = 8192
DEPTH = 1

MIX_WIDTH = D_MODEL
DIFF_HEADS = 8
DIFF_HEAD_DIM = 128
DIFF_QK_DIM = DIFF_HEAD_DIM // 2
SB_HEADS = 8
SB_HEAD_DIM = 128
DIFF_WIDTH = DIFF_HEADS * DIFF_HEAD_DIM
SB_WIDTH = SB_HEADS * SB_HEAD_DIM
IN_COLS = 3 * DIFF_WIDTH + 3 * SB_WIDTH
ROPE_THETA = 500000.0
ROT_DIM = DIFF_QK_DIM // 4
Q_BLOCK = 128
PEER_HEADS = 8
PEER_NKEYS = 128
PEER_N = PEER_NKEYS * PEER_NKEYS
PEER_QDIM = 256
PEER_HALF = PEER_QDIM // 2
PEER_TOPK = 16
PLE_DIM = 256
LN_EPS = 1e-5

kernel_name = "hymba_diff_stickbreak_peer_deepnorm"


def layer_norm(x, g, b):
    xf = x.astype(jnp.float32)
    mu = jnp.mean(xf, axis=-1, keepdims=True)
    xc = xf - mu
    var = jnp.mean(xc * xc, axis=-1, keepdims=True)
    return (xc * lax.rsqrt(var + LN_EPS) * g.astype(jnp.float32) + b.astype(jnp.float32)).astype(x.dtype)


def rope_tables(seq):
    pos = jnp.arange(seq, dtype=jnp.float32)
    inv = ROPE_THETA ** (-jnp.arange(0, ROT_DIM, 2, dtype=jnp.float32) / ROT_DIM)
    ang = pos[:, None] * inv[None, :]
    return jnp.cos(ang), jnp.sin(ang)


def apply_partial_rope(t, cos, sin):
    half = ROT_DIM // 2
    a = t[..., :half]
    b = t[..., half:ROT_DIM]
    c = cos.astype(t.dtype)
    s = sin.astype(t.dtype)
    rot = jnp.concatenate([a * c - b * s, b * c + a * s], axis=-1)
    return jnp.concatenate([rot, t[..., ROT_DIM:]], axis=-1)


def split_heads(t, n_heads):
    b, s, _ = t.shape
    return t.reshape(b, s, n_heads, -1).transpose(0, 2, 1, 3)


def merge_heads(t):
    b, h, s, d = t.shape
    return t.transpose(0, 2, 1, 3).reshape(b, s, h * d)


def to_blocks(t):
    b, h, s, d = t.shape
    nb = s // Q_BLOCK
    return t.reshape(b, h, nb, Q_BLOCK, d).transpose(2, 0, 1, 3, 4)


def from_blocks(o):
    nb, b, h, qb, d = o.shape
    return o.transpose(1, 2, 0, 3, 4).reshape(b, h, nb * qb, d)


def diff_attention(q1, q2, k1, k2, v, lam):
    seq = k1.shape[2]
    nb = seq // Q_BLOCK
    scale = DIFF_QK_DIM ** -0.5
    kpos = jnp.arange(seq)
    k1f = k1.astype(jnp.float32)
    k2f = k2.astype(jnp.float32)
    vf = v.astype(jnp.float32)

    def block(args):
        q1b, q2b, bi = args
        qpos = bi * Q_BLOCK + jnp.arange(Q_BLOCK)
        mask = kpos[None, :] <= qpos[:, None]
        s1 = jnp.einsum('bhqd,bhkd->bhqk', q1b.astype(jnp.float32), k1f) * scale
        s2 = jnp.einsum('bhqd,bhkd->bhqk', q2b.astype(jnp.float32), k2f) * scale
        p1 = jax.nn.softmax(jnp.where(mask, s1, -jnp.inf), axis=-1)
        p2 = jax.nn.softmax(jnp.where(mask, s2, -jnp.inf), axis=-1)
        return jnp.einsum('bhqk,bhkd->bhqd', p1 - lam * p2, vf)

    out = lax.map(block, (to_blocks(q1), to_blocks(q2), jnp.arange(nb)))
    return from_blocks(out)


def stick_breaking_attention(q, k, v):
    seq = k.shape[2]
    nb = seq // Q_BLOCK
    scale = SB_HEAD_DIM ** -0.5
    kpos = jnp.arange(seq)
    kf = k.astype(jnp.float32)
    vf = v.astype(jnp.float32)

    def block(args):
        qb, bi = args
        qpos = bi * Q_BLOCK + jnp.arange(Q_BLOCK)
        valid = kpos[None, :] < qpos[:, None]
        z = jnp.einsum('bhqd,bhkd->bhqk', qb.astype(jnp.float32), kf) * scale
        log_beta = jax.nn.log_sigmoid(z)
        log_keep = jnp.where(valid, jax.nn.log_sigmoid(-z), 0.0)
        suffix = lax.cumsum(log_keep, axis=3, reverse=True) - log_keep
        a = jnp.where(valid, jnp.exp(log_beta + suffix), 0.0)
        return jnp.einsum('bhqk,bhkd->bhqd', a, vf)

    out = lax.map(block, (to_blocks(q), jnp.arange(nb)))
    return from_blocks(out)


def peer_ffn(x, wq, sub_keys, u, v):
    b, s, d = x.shape
    t = b * s
    xt = x.reshape(t, d)
    q = (xt @ wq).reshape(t, PEER_HEADS, PEER_QDIM)
    qa, qb = q[..., :PEER_HALF], q[..., PEER_HALF:]
    sa = jnp.einsum('thc,nc->thn', qa, sub_keys[0]).astype(jnp.float32)
    sb = jnp.einsum('thc,nc->thn', qb, sub_keys[1]).astype(jnp.float32)
    va, ia = lax.top_k(sa, PEER_TOPK)
    vb, ib = lax.top_k(sb, PEER_TOPK)
    cand = (va[..., :, None] + vb[..., None, :]).reshape(t, PEER_HEADS, PEER_TOPK * PEER_TOPK)
    cidx = (ia[..., :, None] * PEER_NKEYS + ib[..., None, :]).reshape(t, PEER_HEADS, PEER_TOPK * PEER_TOPK)
    top, sel = lax.top_k(cand, PEER_TOPK)
    eidx = jnp.take_along_axis(cidx, sel, axis=-1)
    gates = jax.nn.softmax(top, axis=-1)
    nb = t // Q_BLOCK

    def block(args):
        xb, eb, gb = args
        ub = u[eb]
        vbk = v[eb]
        act = jax.nn.gelu(jnp.einsum('thed,td->the', ub, xb).astype(jnp.float32))
        w = (gb * act).astype(xb.dtype)
        return jnp.einsum('the,thed->td', w, vbk)

    out = lax.map(block, (xt.reshape(nb, Q_BLOCK, d),
                          eidx.reshape(nb, Q_BLOCK, PEER_HEADS, PEER_TOPK),
                          gates.reshape(nb, Q_BLOCK, PEER_HEADS, PEER_TOPK)))
    return out.reshape(b, s, d)


def setup_inputs(seed: int = 0) -> dict:
    key = jax.random.key(seed)
    ks = jax.random.split(key, 16)
    f32 = jnp.float32
    beta = (8.0 * DEPTH) ** -0.25
    x = jax.random.normal(ks[0], (BATCH, SEQ, D_MODEL), f32)
    p = jax.random.normal(ks[1], (DEPTH, BATCH, SEQ, PLE_DIM), f32)
    col_scale = jnp.ones((IN_COLS,), f32)
    col_scale = col_scale.at[2 * DIFF_WIDTH:3 * DIFF_WIDTH].set(beta)
    col_scale = col_scale.at[3 * DIFF_WIDTH + 2 * SB_WIDTH:].set(beta)
    w_in = jax.random.normal(ks[2], (DEPTH, D_MODEL, IN_COLS), f32) * (D_MODEL ** -0.5) * col_scale
    w_out = jax.random.normal(ks[3], (DEPTH, MIX_WIDTH, D_MODEL), f32) * (MIX_WIDTH ** -0.5) * beta
    lam = jax.random.normal(ks[4], (4, DEPTH, DIFF_QK_DIM), f32) * 0.1
    diff_subln_g = 1.0 + 0.01 * jax.random.normal(ks[5], (DEPTH, DIFF_HEAD_DIM), f32)
    ln_g = 1.0 + 0.01 * jax.random.normal(ks[6], (DEPTH, 3, D_MODEL), f32)
    ln_b = 0.01 * jax.random.normal(ks[7], (DEPTH, 3, D_MODEL), f32)
    peer_wq = jax.random.normal(ks[8], (DEPTH, D_MODEL, PEER_HEADS * PEER_QDIM), f32) * (D_MODEL ** -0.5)
    peer_keys = jax.random.normal(ks[9], (DEPTH, 2, PEER_NKEYS, PEER_HALF), f32) * (PEER_HALF ** -0.5)
    peer_u = jax.random.normal(ks[10], (DEPTH, PEER_N, D_MODEL), f32) * (D_MODEL ** -0.5)
    peer_v = jax.random.normal(ks[11], (DEPTH, PEER_N, D_MODEL), f32) * (PEER_HEADS ** -0.5) * beta
    ple_gate = jax.random.normal(ks[12], (DEPTH, D_MODEL, D_MODEL), f32) * (D_MODEL ** -0.5)
    ple_proj = jax.random.normal(ks[13], (DEPTH, PLE_DIM, D_MODEL), f32) * (PLE_DIM ** -0.5) * beta
    return {"x": x, "p": p, "w_in": w_in, "w_out": w_out,
            "lambda_q1": lam[0], "lambda_k1": lam[1], "lambda_q2": lam[2], "lambda_k2": lam[3],
            "diff_subln_g": diff_subln_g, "ln_g": ln_g, "ln_b": ln_b,
            "peer_wq": peer_wq, "peer_keys": peer_keys, "peer_u": peer_u, "peer_v": peer_v,
            "ple_gate": ple_gate, "ple_proj": ple_proj}


def reference(x, p, w_in, w_out, lambda_q1, lambda_k1, lambda_q2, lambda_k2, diff_subln_g,
              ln_g, ln_b, peer_wq, peer_keys, peer_u, peer_v, ple_gate, ple_proj):
    alpha = (2.0 * DEPTH) ** 0.25
    seq = x.shape[1]
    cos, sin = rope_tables(seq)
    splits = [DIFF_WIDTH, 2 * DIFF_WIDTH, 3 * DIFF_WIDTH,
              3 * DIFF_WIDTH + SB_WIDTH, 3 * DIFF_WIDTH + 2 * SB_WIDTH]
    for i in range(DEPTH):
        proj = jnp.einsum('bsd,dc->bsc', x, w_in[i])
        dq, dk, dv, sq, sk, sv = jnp.split(proj, splits, axis=-1)
        dq = split_heads(dq, DIFF_HEADS)
        dk = split_heads(dk, DIFF_HEADS)
        dv = split_heads(dv, DIFF_HEADS)
        q1 = apply_partial_rope(dq[..., :DIFF_QK_DIM], cos, sin)
        q2 = apply_partial_rope(dq[..., DIFF_QK_DIM:], cos, sin)
        k1 = apply_partial_rope(dk[..., :DIFF_QK_DIM], cos, sin)
        k2 = apply_partial_rope(dk[..., DIFF_QK_DIM:], cos, sin)
        lambda_init = 0.8 - 0.6 * math.exp(-0.3 * i)
        lam = (jnp.exp(jnp.sum(lambda_q1[i].astype(jnp.float32) * lambda_k1[i].astype(jnp.float32)))
               - jnp.exp(jnp.sum(lambda_q2[i].astype(jnp.float32) * lambda_k2[i].astype(jnp.float32)))
               + lambda_init)
        od = diff_attention(q1, q2, k1, k2, dv, lam)
        od = od * lax.rsqrt(jnp.mean(od * od, axis=-1, keepdims=True) + LN_EPS)
        od = od * diff_subln_g[i].astype(jnp.float32) * (1.0 - lambda_init)
        osb = stick_breaking_attention(split_heads(sq, SB_HEADS), split_heads(sk, SB_HEADS),
                                       split_heads(sv, SB_HEADS))
        merged = jnp.concatenate([merge_heads(od), merge_heads(osb)], axis=-1).astype(x.dtype)
        mix_out = jnp.einsum('bsc,cd->bsd', merged, w_out[i])
        x = layer_norm(alpha * x + mix_out, ln_g[i, 0], ln_b[i, 0])
        ffn_out = peer_ffn(x, peer_wq[i], peer_keys[i], peer_u[i], peer_v[i])
        x = layer_norm(alpha * x + ffn_out, ln_g[i, 1], ln_b[i, 1])
        gate = jax.nn.sigmoid(jnp.einsum('bsd,de->bse', x, ple_gate[i]))
        emb = jnp.einsum('bsk,kd->bsd', p[i], ple_proj[i])
        x = layer_norm(alpha * x + gate * emb, ln_g[i, 2], ln_b[i, 2])
    return x
```

```python
import contextlib
import types
import numpy as np
import concourse.bass as bass
import concourse.mybir as mybir
from concourse.bass_utils import run_bass_kernel_spmd

F32 = mybir.dt.float32
BF16 = mybir.dt.bfloat16
AF = mybir.ActivationFunctionType
ALU = mybir.AluOpType
AX = mybir.AxisListType


def _freeze(fn):
    if getattr(fn, "__closure__", None) is None:
        return fn
    cells = []
    for c in fn.__closure__:
        try:
            cells.append(types.CellType(c.cell_contents))
        except ValueError:
            cells.append(c)
    g = types.FunctionType(fn.__code__, fn.__globals__, fn.__name__, fn.__defaults__, tuple(cells))
    g.__kwdefaults__ = fn.__kwdefaults__
    return g


class Eng:
    def __init__(self, name, serial=True):
        self.name = name
        self.ops = []
        self.n = 0
        self.sem = None
        self.waited = {}
        self.serial = serial

    def wait(self, other, val):
        if val <= 0:
            return
        key = id(other)
        if self.waited.get(key, 0) >= val:
            return
        self.waited[key] = val
        self.ops.append(("wait", other, val))

    def op(self, fn, count=True, nosync=False):
        if count and self.serial and not nosync:
            self.wait(self, self.n)
        self.ops.append(("op", _freeze(fn), count))
        if count:
            self.n += 1
        return self.n

    def dma(self, fn, dsem):
        self.ops.append(("dma", _freeze(fn), dsem))
        dsem.n += 16
        return dsem.n

    def cc(self, fn, dsem):
        self.ops.append(("cc", _freeze(fn), dsem))
        dsem.n += 1
        return dsem.n

    def replay(self, e):
        for o in self.ops:
            if o[0] == "cc":
                ins = o[1](e)
                ins.then_inc(o[2].sem, 1)
            elif o[0] == "wait":
                e.wait_ge(o[1].sem, o[2])
            elif o[0] == "op":
                ins = o[1](e)
                if o[2]:
                    ins.then_inc(self.sem, 1)
            else:
                ins = o[1](e)
                ins.then_inc(o[2].sem, 16)


class DmaSem:
    def __init__(self, name):
        self.name = name
        self.n = 0
        self.sem = None


class Prog:
    def __init__(self):
        self.nc = bass.Bass("TRN2", target_bir_lowering=False)
        self.pe = Eng("pe", serial=False)
        self.act = Eng("act")
        self.dve = Eng("dve")
        self.pool = Eng("pool")
        self.sp = Eng("sp", serial=False)
        self.engs = [self.pe, self.act, self.dve, self.pool, self.sp]
        self.dsems = []
        self.stack = contextlib.ExitStack()

    def dsem(self, name):
        d = DmaSem(name)
        self.dsems.append(d)
        return d

    def sb(self, name, shape, dt):
        return self.stack.enter_context(self.nc.sbuf_tensor(name, list(shape), dt))

    def ps(self, name, shape, dt=F32):
        return self.stack.enter_context(self.nc.psum_tensor(name, list(shape), dt))

    def dram(self, name, shape, dt, kind):
        return self.nc.dram_tensor(name, list(shape), dt, kind=kind)

    def finish(self):
        nc = self.nc
        for e in self.engs:
            e.sem = self.stack.enter_context(nc.semaphore("sem_" + e.name))
        for d in self.dsems:
            d.sem = self.stack.enter_context(nc.semaphore("ds_" + d.name))
        with nc.Block() as block:
            @block.tensor
            def _(t):
                self.pe.replay(t)

            @block.scalar
            def _(t):
                self.act.replay(t)

            @block.vector
            def _(t):
                self.dve.replay(t)

            @block.gpsimd
            def _(t):
                self.pool.replay(t)

            @block.sync
            def _(t):
                self.sp.replay(t)
        self.stack.close()
        return nc


S = 8192
NT = S // 128
NB = S // 512
D = 2048
NCH = D // 128


def build_phase_a(P, out_attn, NQB=NB):
    nc = P.nc
    pe, act, dve, pool, sp = P.pe, P.act, P.dve, P.pool, P.sp
    xT = P.dram("xT", [D, S], F32, "ExternalInput")
    wA = P.dram("wA", [D, 1024], F32, "ExternalInput")
    ropeC = P.dram("ropeC", [128, S], F32, "ExternalInput")
    ropeS = P.dram("ropeS", [128, S], F32, "ExternalInput")
    lamv = P.dram("lamv", [1, 256], F32, "ExternalInput")
    gsub = P.dram("gsub", [1, 128], F32, "ExternalInput")

    xTv = xT.ap().rearrange("(c p) t -> p c t", p=128)
    wAv = wA.ap().rearrange("(c p) n -> p c n", p=128)

    wsb = P.sb("wsb", [128, NCH, 1024], BF16)
    xb = [P.sb(f"xb{i}", [128, NCH, 512], BF16) for i in range(2)]
    cb = [P.sb(f"cb{i}", [128, 512], F32) for i in range(2)]
    sbb = [P.sb(f"sbb{i}", [128, 512], F32) for i in range(2)]
    QT = P.sb("QT", [128, S], BF16)
    KT = P.sb("KT", [128, S], BF16)
    QsT = P.sb("QsT", [128, S], BF16)
    KsT = P.sb("KsT", [128, S], BF16)
    Vd = P.sb("Vd", [128, NT, 130], BF16)
    Vs = P.sb("Vs", [128, NT, 128], BF16)
    t1 = P.sb("t1", [128, 512], F32)
    t2 = P.sb("t2", [128, 512], F32)
    cst = P.sb("cst", [128, 4, 128], BF16)
    cstf = P.sb("cstf", [128, 128], F32)
    lamt = P.sb("lamt", [128, 256], F32)
    lamw = P.sb("lamw", [128, 128], F32)
    lams = P.sb("lams", [128, 8], F32)
    g08 = P.sb("g08", [128, 128], F32)
    PT1 = [xb[0][:, 0 + i, :] for i in range(2)]
    PT2 = [xb[0][:, 2 + i, :] for i in range(2)]
    fa = P.sb("fa", [128, 128], F32)
    fod = P.sb("fod", [128, 128], F32)
    fjunk = P.sb("fjunk", [128, 128], F32)
    fst = P.sb("fst", [128, 8], F32)
    oout = [P.sb(f"oout{i}", [128, 128], BF16) for i in range(2)]
    eb = [cb[0], cb[1]]
    spb = [sbb[0], sbb[1]]
    nkb = [xb[0][:, 4 + i, :] for i in range(2)]
    Sacc = P.sb("Sacc", [128, 512], F32)
    Sbf = [xb[0][:, 6 + i, :] for i in range(2)]
    argb = [t1, t2]
    Ab = [xb[0][:, 8 + i, :] for i in range(2)]

    ps = P.ps("psA", [128, 8, 512], F32)

    d_w = P.dsem("w")
    d_x = [P.dsem("x0"), P.dsem("x1")]
    d_cs = [P.dsem("cs0"), P.dsem("cs1")]
    d_misc = P.dsem("misc")
    d_out = [P.dsem("o0"), P.dsem("o1")]

    sp.dma(lambda e: e.dma_start(out=lamt[:], in_=lamv.ap().partition_broadcast(128)), d_misc)
    sp.dma(lambda e: e.dma_start(out=g08[:], in_=gsub.ap().partition_broadcast(128)), d_misc)
    def mk(idx, pattern_step, cm, cmp, base=0):
        pool.op(lambda e: e.memset(cstf[:], 1.0))
        if cmp is not None:
            pool.op(lambda e: e.affine_select(out=cstf[:], in_=cstf[:], pattern=[[pattern_step, 128]],
                                              compare_op=cmp, fill=0.0, base=base, channel_multiplier=cm))
        pool.op(lambda e: e.tensor_copy(out=cst[:, idx, :], in_=cstf[:]))
    mk(0, 1, -1, ALU.is_ge)
    mk(1, 1, -1, ALU.is_gt)
    mk(2, -1, 1, ALU.is_gt)
    mk(3, 0, 0, None)
    pool.op(lambda e: e.memset(Vd[:, :, 128:130], 1.0))
    cst_done = pool.n
    maskLE = cst[:, 0, :]
    maskLT = cst[:, 1, :]
    triGT = cst[:, 2, :]
    ones = cst[:, 3, :]

    dve.wait(d_misc, 32)
    dve.op(lambda e: e.tensor_tensor(out=lamw[:, 0:64], in0=lamt[:, 0:64], in1=lamt[:, 64:128], op=ALU.mult))
    dve.op(lambda e: e.tensor_tensor(out=lamw[:, 64:128], in0=lamt[:, 128:192], in1=lamt[:, 192:256], op=ALU.mult))
    dve.op(lambda e: e.reduce_sum(out=lams[:, 0:2], in_=lamw[:].rearrange("p (a b) -> p a b", a=2), axis=AX.X))
    i_ls = dve.n
    act.wait(dve, i_ls)
    act.op(lambda e: e.activation(out=lams[:, 2:4], in_=lams[:, 0:2], func=AF.Exp))
    i_le = act.n
    dve.wait(act, i_le)
    dve.op(lambda e: e.scalar_tensor_tensor(out=lams[:, 4:5], in0=lams[:, 3:4], scalar=-0.2, in1=lams[:, 2:3],
                                            op0=ALU.add, op1=ALU.subtract))
    dve.op(lambda e: e.tensor_scalar(out=g08[:], in0=g08[:], scalar1=0.8, scalar2=None, op0=ALU.mult))
    neglam = lams[:, 4:5]

    for c4 in range(4):
        pool.dma(lambda e, c4=c4: e.dma_start(out=wsb[:, 4 * c4:4 * c4 + 4, :], in_=wAv[:, 4 * c4:4 * c4 + 4, :]), d_w)
    pe_blk_done = [0] * NB
    pe_g1 = [0] * NB
    pe_g2 = [0] * NB
    dve_g1 = [0] * NB
    act_g2 = [0] * NB
    for b in range(NB):
        bi = b % 2
        tok = slice(b * 512, (b + 1) * 512)
        if b >= 2:
            pool.wait(pe, pe_blk_done[b - 2])
            sp.wait(dve, dve_g1[b - 2])
        for h in range(2):
            pool.dma(lambda e, bi=bi, h=h, tok=tok: e.dma_start(out=xb[bi][:, 8 * h:8 * h + 8, :], in_=xTv[:, 8 * h:8 * h + 8, tok]), d_x[bi])
        sp.dma(lambda e, bi=bi, tok=tok: e.dma_start(out=cb[bi][:], in_=ropeC.ap()[:, tok]), d_cs[bi])
        sp.dma(lambda e, bi=bi, tok=tok: e.dma_start(out=sbb[bi][:], in_=ropeS.ap()[:, tok]), d_cs[bi])
        xval = d_x[bi].n
        csval = d_cs[bi].n
        pe.wait(d_w, 64)
        pe.wait(d_x[bi], xval)
        if b >= 1:
            pe.wait(dve, dve_g1[b - 1])
        for g, col in enumerate([0, 512, 128, 640]):
            for ch in range(NCH):
                fn = lambda e, g=g, col=col, ch=ch, bi=bi: e.matmul(ps[:, g, :], lhsT=wsb[:, ch, col:col + 128], rhs=xb[bi][:, ch, :],
                                                                    start=(ch == 0), stop=(ch == NCH - 1))
                last = (g == 3 and ch == NCH - 1)
                r = pe.op(fn, count=last)
        pe_g1[b] = pe.n
        if b >= 1:
            pe.wait(act, act_g2[b - 1])
        for g, col in [(4, 256), (5, 384)]:
            for ch in range(NCH):
                pe.op(lambda e, g=g, col=col, ch=ch, bi=bi: e.matmul(ps[:, g, :], lhsT=wsb[:, ch, col:col + 128], rhs=xb[bi][:, ch, :],
                                                                     start=(ch == 0), stop=(ch == NCH - 1)), count=False)
        for t in range(4):
            for ch in range(NCH):
                last = (t == 3 and ch == NCH - 1)
                pe.op(lambda e, t=t, ch=ch, bi=bi: e.matmul(ps[:, 6 + t // 2, (t % 2) * 256:(t % 2) * 256 + 256],
                                                            lhsT=xb[bi][:, ch, t * 128:(t + 1) * 128], rhs=wsb[:, ch, 768:1024],
                                                            start=(ch == 0), stop=(ch == NCH - 1)), count=last)
        pe_g2[b] = pe.n
        pe_blk_done[b] = pe.n
        dve.wait(pe, pe_g1[b])
        dve.wait(d_cs[bi], csval)
        for (bq, bs, dst) in [(0, 1, QT), (2, 3, KT)]:
            dve.op(lambda e, bq=bq, bi=bi: e.tensor_tensor(out=t1[:], in0=ps[:, bq, :], in1=cb[bi][:], op=ALU.mult))
            dve.op(lambda e, bs=bs, bi=bi: e.tensor_tensor(out=t2[:], in0=ps[:, bs, :], in1=sbb[bi][:], op=ALU.mult))
            dve.op(lambda e, dst=dst, tok=tok: e.tensor_tensor(out=dst[:, tok], in0=t1[:], in1=t2[:], op=ALU.add))
        dve_g1[b] = dve.n
        act.wait(pe, pe_g2[b])
        act.op(lambda e, tok=tok: e.copy(out=QsT[:, tok], in_=ps[:, 4, :]))
        act.op(lambda e, tok=tok: e.copy(out=KsT[:, tok], in_=ps[:, 5, :]))
        for h in range(2):
            src = ps[:, 6 + h, :].rearrange("p (t c) -> p t c", t=2)
            act.op(lambda e, src=src, h=h, b=b: e.copy(out=Vd[:, 4 * b + 2 * h:4 * b + 2 * h + 2, 0:128], in_=src[:, :, 0:128]))
            act.op(lambda e, src=src, h=h, b=b: e.copy(out=Vs[:, 4 * b + 2 * h:4 * b + 2 * h + 2, :], in_=src[:, :, 128:256]))
        act_g2[b] = act.n
    proj_dve = dve.n
    proj_act = act.n
    proj_pe = pe.n

    steps = []
    for qb in range(NQB):
        for j in range(4 * qb + 4):
            steps.append((qb, j))
    ns = len(steps)
    SC_D = 0.125
    act_exp = [0] * ns
    dve_mask = [0] * ns
    pe_qk = [0] * ns
    pe_pv = [0] * ns
    dve_fin = {}
    pe.wait(dve, proj_dve)
    pe.wait(act, proj_act)
    pe.wait(pool, cst_done)
    act.wait(pe, proj_pe)
    dve.wait(pe, proj_pe)
    dve.wait(pool, cst_done)
    n_out = 0

    def emit_qk(s):
        qb, j = steps[s]
        bi = s % 2
        c0 = max(0, j - 4 * qb) * 128
        q0 = qb * 512
        if s >= 2:
            pe.wait(act, act_exp[s - 2])
        pe.op(lambda e: e.matmul(ps[:, bi, c0:512], lhsT=KT[0:64, j * 128:(j + 1) * 128], rhs=QT[0:64, q0 + c0:q0 + 512],
                                 start=True, stop=True), count=False)
        pe_qk[s] = pe.op(lambda e: e.matmul(ps[:, 2 + bi, c0:512], lhsT=KT[64:128, j * 128:(j + 1) * 128], rhs=QT[64:128, q0 + c0:q0 + 512],
                                            start=True, stop=True))

    def emit_rest(s):
        nonlocal n_out
        qb, j = steps[s]
        bi = s % 2
        d = j - 4 * qb
        c0 = max(0, d) * 128
        act.wait(pe, pe_qk[s])
        if s >= 2:
            act.wait(pe, pe_pv[s - 2])
            if dve_mask[s - 2]:
                act.wait(dve, dve_mask[s - 2])
        act.op(lambda e: e.activation(out=PT1[bi][:, c0:512], in_=ps[:, bi, c0:512], func=AF.Exp, scale=SC_D))
        act_exp[s] = act.op(lambda e: e.activation(out=PT2[bi][:, c0:512], in_=ps[:, 2 + bi, c0:512], func=AF.Exp, scale=SC_D))
        if d >= 0:
            dve.wait(act, act_exp[s])
            dve.op(lambda e: e.tensor_tensor(out=PT1[bi][:, c0:c0 + 128], in0=PT1[bi][:, c0:c0 + 128], in1=maskLE, op=ALU.mult))
            dve_mask[s] = dve.op(lambda e: e.tensor_tensor(out=PT2[bi][:, c0:c0 + 128], in0=PT2[bi][:, c0:c0 + 128], in1=maskLE, op=ALU.mult))
        pe.wait(act, act_exp[s])
        if d >= 0:
            pe.wait(dve, dve_mask[s])
        if j == 0 and qb >= 1:
            for t in range(4):
                pe.wait(dve, dve_fin[(qb - 1, t)])
        tl = list(range(max(0, d), 4))
        for t in tl:
            lastj = (j == 4 * qb + t)
            pe.op(lambda e, t=t, lastj=lastj: e.matmul(ps[:, 4 + t, 0:129], lhsT=PT1[bi][:, t * 128:(t + 1) * 128], rhs=Vd[:, j, 0:129],
                                                       start=(j == 0), stop=lastj, skip_group_check=True), count=False)
            cnt = (t == tl[-1])
            r = pe.op(lambda e, t=t, lastj=lastj: e.matmul(ps[:, 4 + t, 256:385], lhsT=PT2[bi][:, t * 128:(t + 1) * 128], rhs=Vd[:, j, 0:129],
                                                           start=False, stop=lastj, skip_group_check=True), count=cnt)
        pe_pv[s] = pe.n
        if d >= 0:
            t = d
            qt = 4 * qb + t
            O1 = ps[:, 4 + t, 0:128]
            O2 = ps[:, 4 + t, 256:384]
            dve.wait(pe, pe_pv[s])
            dve.op(lambda e: e.reciprocal(out=fst[:, 0:1], in_=ps[:, 4 + t, 128:129]))
            dve.op(lambda e: e.reciprocal(out=fst[:, 1:2], in_=ps[:, 4 + t, 384:385]))
            dve.op(lambda e: e.tensor_tensor(out=fst[:, 2:3], in0=fst[:, 1:2], in1=neglam, op=ALU.mult))
            dve.op(lambda e: e.tensor_scalar(out=fa[:], in0=O1, scalar1=fst[:, 0:1], scalar2=None, op0=ALU.mult))
            dve_fin[(qb, t)] = dve.op(lambda e: e.scalar_tensor_tensor(out=fod[:], in0=O2, scalar=fst[:, 2:3], in1=fa[:], op0=ALU.mult, op1=ALU.add))
            i1 = dve.n
            act.wait(dve, i1)
            act.op(lambda e: e.activation(out=fjunk[:], in_=fod[:], func=AF.Square, accum_out=fst[:, 3:4]))
            act.op(lambda e: e.activation(out=fst[:, 4:5], in_=fst[:, 3:4], func=AF.Ln, bias=1e-5, scale=1.0 / 128))
            act.op(lambda e: e.activation(out=fst[:, 5:6], in_=fst[:, 4:5], func=AF.Exp, scale=-0.5))
            i2 = act.n
            dve.wait(act, i2)
            ob = n_out % 2
            if n_out >= 2:
                dve.wait(d_out[ob], 16 * (n_out // 2))
            dve.op(lambda e, ob=ob: e.scalar_tensor_tensor(out=oout[ob][:], in0=fod[:], scalar=fst[:, 5:6], in1=g08[:], op0=ALU.mult, op1=ALU.mult))
            i3 = dve.n
            sp.wait(dve, i3)
            sp.dma(lambda e, ob=ob, qt=qt: e.dma_start(out=out_attn[qt * 128:(qt + 1) * 128, 0:128], in_=oout[ob][:]), d_out[ob])
            n_out += 1

    emit_qk(0)
    for s in range(ns):
        if s + 1 < ns:
            emit_qk(s + 1)
        emit_rest(s)
    diff_pe = pe.n
    diff_act = act.n
    diff_dve = dve.n

    SC_S = 128 ** -0.5
    steps = []
    for qb in range(NQB):
        for j in range(4 * qb + 3, -1, -1):
            steps.append((qb, j))
    ns = len(steps)
    pe_z = [0] * ns
    pe_c = [0] * ns
    pe_av = [0] * ns
    act_sp = [0] * ns
    act_A = [0] * ns
    dve_nk = [0] * ns
    dve_arg = [0] * ns
    dve_mA = [0] * ns
    pool_S = [0] * ns
    fin_sb = {}
    pe.wait(dve, diff_dve)
    pe.wait(act, diff_act)
    act.wait(pe, diff_pe)
    dve.wait(pe, diff_pe)

    def info(s):
        qb, j = steps[s]
        d = j - 4 * qb
        c0 = max(0, d) * 128
        first = (j == 4 * qb + 3)
        return qb, j, d, c0, first

    def sb_qk(s):
        qb, j, d, c0, first = info(s)
        bi = s % 2
        q0 = qb * 512
        if s >= 2:
            pe.wait(dve, dve_nk[s - 2])
        pe_z[s] = pe.op(lambda e: e.matmul(ps[:, bi, c0:512], lhsT=KsT[:, j * 128:(j + 1) * 128], rhs=QsT[:, q0 + c0:q0 + 512],
                                           start=True, stop=True))

    def sb_front(s):
        qb, j, d, c0, first = info(s)
        bi = s % 2
        act.wait(pe, pe_z[s])
        if s >= 2:
            act.wait(dve, dve_arg[s - 2])
        act.op(lambda e: e.activation(out=eb[bi][:, c0:512], in_=ps[:, bi, c0:512], func=AF.Exp, scale=-SC_S))
        act_sp[s] = act.op(lambda e: e.activation(out=spb[bi][:, c0:512], in_=eb[bi][:, c0:512], func=AF.Ln, bias=1.0, scale=1.0))
        dve.wait(act, act_sp[s])
        if s >= 2:
            dve.wait(pe, pe_c[s - 2])
            dve.wait(pool, pool_S[s - 2])
        dve.op(lambda e: e.scalar_tensor_tensor(out=nkb[bi][:, c0:512], in0=ps[:, bi, c0:512], scalar=SC_S, in1=spb[bi][:, c0:512],
                                                op0=ALU.mult, op1=ALU.add))
        if d >= 0:
            dve.op(lambda e: e.tensor_tensor(out=nkb[bi][:, c0:c0 + 128], in0=nkb[bi][:, c0:c0 + 128], in1=maskLT, op=ALU.mult))
        dve_nk[s] = dve.n
        pool.wait(dve, dve_nk[s])
        if first:
            pool.op(lambda e: e.memset(Sacc[:], 0.0))
        pool.op(lambda e: e.tensor_tensor(out=Sacc[:, c0:512], in0=Sacc[:, c0:512], in1=nkb[bi][:, c0:512], op=ALU.add))
        if s >= 1:
            pool.wait(pe, pe_c[s - 1])
        pool_S[s] = pool.op(lambda e: e.tensor_copy(out=Sbf[bi][:], in_=Sacc[:]))

    def sb_cum(s):
        qb, j, d, c0, first = info(s)
        bi = s % 2
        pe.wait(dve, dve_nk[s])
        if s >= 2:
            pe.wait(dve, dve_arg[s - 2])
        if first:
            pe_c[s] = pe.op(lambda e: e.matmul(ps[:, 2 + bi, c0:512], lhsT=triGT, rhs=nkb[bi][:, c0:512], start=True, stop=True))
        else:
            pe.op(lambda e: e.matmul(ps[:, 2 + bi, c0:512], lhsT=triGT, rhs=nkb[bi][:, c0:512], start=True, stop=False), count=False)
            pe.wait(pool, pool_S[s - 1])
            pe_c[s] = pe.op(lambda e: e.matmul(ps[:, 2 + bi, c0:512], lhsT=ones, rhs=Sbf[1 - bi][:, c0:512], start=False, stop=True))

    def sb_back(s):
        nonlocal n_out
        qb, j, d, c0, first = info(s)
        bi = s % 2
        dve.wait(pe, pe_c[s])
        if s >= 2:
            dve.wait(act, act_A[s - 2])
        dve_arg[s] = dve.op(lambda e: e.tensor_tensor(out=argb[bi][:, c0:512], in0=ps[:, 2 + bi, c0:512], in1=spb[bi][:, c0:512], op=ALU.add))
        act.wait(dve, dve_arg[s])
        if s >= 2:
            act.wait(pe, pe_av[s - 2])
            if dve_mA[s - 2]:
                act.wait(dve, dve_mA[s - 2])
        act_A[s] = act.op(lambda e: e.activation(out=Ab[bi][:, c0:512], in_=argb[bi][:, c0:512], func=AF.Exp, scale=-1.0))
        if d >= 0:
            dve.wait(act, act_A[s])
            dve_mA[s] = dve.op(lambda e: e.tensor_tensor(out=Ab[bi][:, c0:c0 + 128], in0=Ab[bi][:, c0:c0 + 128], in1=maskLT, op=ALU.mult))
        pe.wait(act, act_A[s])
        if d >= 0:
            pe.wait(dve, dve_mA[s])
        if first and qb >= 1:
            for t in range(4):
                pe.wait(act, fin_sb[(qb - 1, t)])
        tl = list(range(max(0, d), 4))
        for t in tl:
            pe.op(lambda e, t=t: e.matmul(ps[:, 4 + t, 0:128], lhsT=Ab[bi][:, t * 128:(t + 1) * 128], rhs=Vs[:, j, :],
                                          start=(j == 4 * qb + t), stop=(j == 0)), count=(t == tl[-1]))
        pe_av[s] = pe.n
        if j == 0:
            for t in range(4):
                qt = 4 * qb + t
                ob = n_out % 2
                act.wait(pe, pe_av[s])
                if n_out >= 2:
                    act.wait(d_out[ob], 16 * (n_out // 2))
                fin_sb[(qb, t)] = act.op(lambda e, t=t, ob=ob: e.copy(out=oout[ob][:], in_=ps[:, 4 + t, 0:128]))
                sp.wait(act, fin_sb[(qb, t)])
                sp.dma(lambda e, ob=ob, qt=qt: e.dma_start(out=out_attn[qt * 128:(qt + 1) * 128, 128:256], in_=oout[ob][:]), d_out[ob])
                n_out += 1

    sb_qk(0)
    sb_front(0)
    for s in range(ns):
        if s + 1 < ns:
            sb_qk(s + 1)
        sb_cum(s)
        if s + 1 < ns:
            sb_front(s + 1)
        sb_back(s)
    sp.wait(d_out[0], d_out[0].n)
    sp.wait(d_out[1], d_out[1].n)
    return dict(pe=pe.n, act=act.n, dve=dve.n, pool=pool.n)


TT = 8
ALPHA = 2.0 ** 0.25
NE = 16384
LN_EPS = 1e-5
GELU = AF.Gelu


def build_phase_b(P, attn_in, wait_attn=None, NBLK=64, stop_after=None, gather=None):
    nc = P.nc
    pe, act, dve, pool, sp = P.pe, P.act, P.dve, P.pool, P.sp
    xs = P.dram("xs", [1024, D], F32, "ExternalInput")
    woutP = P.dram("woutP", [D, D], F32, "ExternalInput")
    lng = P.dram("lng", [3, D], F32, "ExternalInput")
    lnb = P.dram("lnb", [3, D], F32, "ExternalInput")
    wq = P.dram("wq", [D, D], F32, "ExternalInput")
    keysT = P.dram("keysT", [2, 128, 128], F32, "ExternalInput")
    uT = P.dram("uT", [D, NE], F32, "ExternalInput")
    vv = P.dram("vv", [NE, D], F32, "ExternalInput")
    wgate = P.dram("wgate", [D, D], F32, "ExternalInput")
    wproj = P.dram("wproj", [256, D], F32, "ExternalInput")
    pT = P.dram("pT", [256, 1024], F32, "ExternalInput")
    outB = P.dram("outB", [1024, D], F32, "ExternalOutput")

    X1 = P.sb("X1", [128, TT, D], F32)
    XT = P.sb("XT", [128, NCH, 1024], BF16)
    W0 = P.sb("W0", [128, 8192], BF16)
    W1 = P.sb("W1", [128, 8192], BF16)
    U = P.sb("U", [128, 8192], F32)
    identf = P.sb("identf", [128, 128], F32)
    identb = P.sb("identb", [128, 128], BF16)
    keys_sb = P.sb("keys_sb", [128, 2, 128], F32)
    st = P.sb("st", [128, 16], F32)
    gelT = [P.sb(f"gelT{i}", [128, 512], F32) for i in range(2)]
    qTc = gelT
    sE = [P.sb(f"sE{i}", [128, 8, 128], F32) for i in range(2)]
    gE = [P.sb(f"gE{i}", [128, 8, 128], BF16) for i in range(2)]
    diag = P.sb("diag", [128, 4, 8, 128], BF16)
    WT = [[P.sb(f"WT{a}{b}", [128, 512], BF16) for b in range(2)] for a in range(2)]
    swork = P.sb("swork", [128, 128], F32)
    ejunk = P.sb("ejunk", [128, 16], F32)
    cand = gelT[0][:, 0:484].rearrange("p (a b) -> p a b", a=22)
    cwork = gelT[1][:, 0:484]
    tk = P.sb("tk", [128, 80], F32)
    hst = P.sb("hst", [128, 4, 8, 8], F32)

    gb = U[:, 0:2048]
    bb = U[:, 2048:4096]
    attn_st = U[:, 4096:6144]
    sig = U[:, 4096:4608]
    tmpb = U[:, 4608:5120]
    junk = W0[:, 0:2048]
    pTf = U[:, 0:2048].rearrange("p (k t) -> p k t", k=2)
    projf = [U[:, 2048:4096], U[:, 5120:7168]]
    sbv = U[:, 0:4096].rearrange("p (t h n) -> p t h n", t=4, h=8)
    bav = U[:, 4096:8192].rearrange("p (t h n) -> p t h n", t=4, h=8)

    ps = P.ps("psB", [128, 8, 512], F32)

    d_xs = P.dsem("bxs")
    d_at = P.dsem("bat")
    d_ln = P.dsem("bln")
    d_w = [P.dsem("bw0"), P.dsem("bw1")]
    d_u = [P.dsem("bu0"), P.dsem("bu1")]
    d_v = [P.dsem("bv0"), P.dsem("bv1")]
    d_misc = P.dsem("bmisc")
    d_o = P.dsem("bout")

    W0v = W0[:].rearrange("p (c n) -> p c n", c=NCH)
    W1v = W1[:].rearrange("p (c n) -> p c n", c=NCH)
    Wv = [W0v, W1v]
    uTb = [W0[:, 0:4096].rearrange("p (c n) -> p c n", c=NCH), W0[:, 4096:8192].rearrange("p (c n) -> p c n", c=NCH)]
    vb = [W1[:, 0:4096].rearrange("p (k d) -> p k d", k=2), W1[:, 4096:8192].rearrange("p (k d) -> p k d", k=2)]

    pool.op(lambda e: e.memset(identf[:], 1.0))
    pool.op(lambda e: e.affine_select(out=identf[:], in_=identf[:], pattern=[[-1, 128]], compare_op=ALU.is_equal,
                                      fill=0.0, base=0, channel_multiplier=1))
    pool.op(lambda e: e.tensor_copy(out=identb[:], in_=identf[:]))
    ident_done = pool.n
    sp.dma(lambda e: e.dma_start(out=keys_sb[:], in_=keysT.ap().rearrange("s c n -> c s n")), d_misc)
    MISC = 16
    for h in range(4):
        sp.dma(lambda e, h=h: e.dma_start(out=X1[:, 2 * h:2 * h + 2, :],
                                          in_=xs.ap().rearrange("(t p) d -> p t d", p=128)[:, 2 * h:2 * h + 2, :]), d_xs)

    wstate = {"n": 0, "pe_done": {}}

    def load_wblock(src_ap_fn):
        n = wstate["n"]
        par = n % 2
        if n >= 2:
            pool.wait(pe, wstate["pe_done"][n - 2])
        for h in range(2):
            pool.dma(lambda e, h=h, par=par: e.dma_start(out=Wv[par][:, 8 * h:8 * h + 8, :], in_=src_ap_fn(h)), d_w[par])
        wstate["n"] += 1
        return n, par, d_w[par].n

    def wsrc(dram, nb):
        v = dram.ap().rearrange("(c p) n -> p c n", p=128)
        return lambda h: v[:, 8 * h:8 * h + 8, nb * 512:(nb + 1) * 512]

    def load_ln(k):
        sp.dma(lambda e: e.dma_start(out=gb, in_=lng.ap()[k:k + 1, :].partition_broadcast(128)), d_ln)
        sp.dma(lambda e: e.dma_start(out=bb, in_=lnb.ap()[k:k + 1, :].partition_broadcast(128)), d_ln)
        return d_ln.n

    def layer_norm(tt, lnval):
        y = X1[:, tt, :]
        dve.op(lambda e: e.reduce_sum(out=st[:, 0:1], in_=y, axis=AX.X))
        dve.op(lambda e: e.tensor_scalar(out=st[:, 1:2], in0=st[:, 0:1], scalar1=-1.0 / D, scalar2=None, op0=ALU.mult))
        i0 = dve.n
        act.wait(dve, i0)
        act.op(lambda e: e.activation(out=junk, in_=y, func=AF.Square, bias=st[:, 1:2], scale=1.0, accum_out=st[:, 2:3]))
        act.op(lambda e: e.activation(out=st[:, 3:4], in_=st[:, 2:3], func=AF.Ln, bias=LN_EPS, scale=1.0 / D))
        act.op(lambda e: e.activation(out=st[:, 4:5], in_=st[:, 3:4], func=AF.Exp, scale=-0.5))
        i1 = act.n
        dve.wait(act, i1)
        dve.wait(d_ln, lnval)
        dve.op(lambda e: e.tensor_scalar(out=y, in0=y, scalar1=st[:, 1:2], scalar2=st[:, 4:5], op0=ALU.add, op1=ALU.mult))
        dve.op(lambda e: e.tensor_tensor(out=y, in0=y, in1=gb, op=ALU.mult))
        return dve.op(lambda e: e.tensor_tensor(out=y, in0=y, in1=bb, op=ALU.add))

    tr_state = {"n": 0, "copy": [0, 0]}

    def transpose_tile(src_fn, tt, after):
        pe.wait(after[0], after[1])
        pe.wait(pool, ident_done)
        for g in range(4):
            n = tr_state["n"]
            bk = n % 2
            if n >= 2:
                pe.wait(act, tr_state["copy"][bk])
            for q in range(4):
                ch = 4 * g + q
                r = pe.op(lambda e, ch=ch, q=q, bk=bk: e.transpose(out=ps[:, bk, q * 128:(q + 1) * 128], in_=src_fn(ch), identity=identf[:]),
                          count=(q == 3))
            act.wait(pe, r)
            tr_state["copy"][bk] = act.op(lambda e, g=g, bk=bk: e.copy(out=XT[:, 4 * g:4 * g + 4, tt * 128:(tt + 1) * 128],
                                                                      in_=ps[:, bk, :].rearrange("p (q n) -> p q n", q=4)))
            tr_state["n"] += 1
        return tr_state["copy"][(tr_state["n"] - 1) % 2]

    ln1 = load_ln(0)
    if gather is None:
        if wait_attn is not None:
            sp.wait(wait_attn[0], wait_attn[1])
        for tt in range(TT):
            if tt >= 1:
                sp.wait(pe, tr_last_pe)
            sp.dma(lambda e, tt=tt: e.dma_start(out=attn_st.rearrange("p (s f) -> p s f", s=8),
                                                in_=attn_in.rearrange("s t f -> t s f")[tt * 128:(tt + 1) * 128, :, :]), d_at)
            last_copy = transpose_tile(lambda ch: attn_st[:, ch * 128:(ch + 1) * 128], tt, (d_at, d_at.n))
            tr_last_pe = pe.n
    else:
        I32 = mybir.dt.int32
        idx_sb = P.sb("idx_sb", [128, 64], I32)
        stage = [WT[q // 2][q % 2][:].rearrange("p (s f) -> p s f", s=2) for q in range(4)]
        d_idx = P.dsem("bidx")
        pool.dma(lambda e: e.dma_start(out=idx_sb[:], in_=gather), d_idx)
        pool.wait(d_idx, 16)
        pool.wait(wait_attn[0], wait_attn[1])
        cp = 0
        for tt in range(TT):
            if tt >= 1:
                pool.wait(dve, cp)
            for s_ in range(8):
                pool.dma(lambda e, tt=tt, s_=s_: e.indirect_dma_start(
                    out=stage[s_ // 2][:, s_ % 2, :], out_offset=None, in_=attn_in,
                    in_offset=bass.IndirectOffsetOnAxis(ap=idx_sb[:, tt * 8 + s_:tt * 8 + s_ + 1], axis=0)), d_at)
            dve.wait(d_at, d_at.n)
            if tt >= 1:
                dve.wait(pe, tr_last_pe)
            for q in range(4):
                cp = dve.op(lambda e, q=q: e.tensor_copy(out=attn_st[:, q * 512:(q + 1) * 512], in_=WT[q // 2][q % 2][:]))
            last_copy = transpose_tile(lambda ch: attn_st[:, ch * 128:(ch + 1) * 128], tt, (dve, cp))
            tr_last_pe = pe.n
    blocks = [load_wblock(wsrc(woutP, 0)), load_wblock(wsrc(woutP, 1))]
    pe.wait(act, last_copy)
    dve.wait(d_xs, 64)
    mm_bank = 0
    dve_mm = {}
    for nb in range(4):
        n, par, dval = blocks[nb]
        pe.wait(d_w[par], dval)
        for tt in range(TT):
            bk = 4 + (mm_bank % 4)
            if mm_bank >= 4:
                pe.wait(dve, dve_mm[mm_bank - 4])
            for ch in range(NCH):
                r = pe.op(lambda e, ch=ch, tt=tt, par=par, bk=bk: e.matmul(ps[:, bk, :], lhsT=XT[:, ch, tt * 128:(tt + 1) * 128], rhs=Wv[par][:, ch, :],
                                                                          start=(ch == 0), stop=(ch == NCH - 1)), count=(ch == NCH - 1))
            dve.wait(pe, r)
            dve_mm[mm_bank] = dve.op(lambda e, tt=tt, nb=nb, bk=bk: e.scalar_tensor_tensor(
                out=X1[:, tt, nb * 512:(nb + 1) * 512], in0=X1[:, tt, nb * 512:(nb + 1) * 512], scalar=ALPHA, in1=ps[:, bk, :],
                op0=ALU.mult, op1=ALU.add))
            mm_bank += 1
        wstate["pe_done"][n] = pe.n
        if nb + 2 < 4:
            blocks.append(load_wblock(wsrc(woutP, nb + 2)))
    ln_idx = []
    for tt in range(TT):
        i = layer_norm(tt, ln1)
        ln_idx.append(i)
        last_copy = transpose_tile(lambda ch, tt=tt: X1[:, tt, ch * 128:(ch + 1) * 128], tt, (dve, i))
    x1_done_dve = dve.n
    if stop_after == "ln1":
        return finish_out(P, X1, outB, d_o, (dve, x1_done_dve))

    dve.wait(pe, pe.n)
    for tt in range(TT):
        dve.op(lambda e, tt=tt: e.tensor_scalar(out=X1[:, tt, :], in0=X1[:, tt, :], scalar1=ALPHA, scalar2=None, op0=ALU.mult))
    peer_state = {"B": 0, "pe_blk": {}, "act_gel": {}, "dve_wt": {}, "dve_accb": [0, 0, 0, 0], "r": 0, "dve_g": {}, "pe_gt": {}, "act_e": {}, "pool_s": {}}
    uTv = uT.ap().rearrange("(c p) e -> p c e", p=128)
    vvv = vv.ap().rearrange("(b k p) d -> b p k d", k=2, p=128)

    def issue_peer_dma(B):
        eb = B % 64
        par = B % 2
        if B >= 2:
            pool.wait(pe, peer_state["pe_blk"][B - 2])
        for h in range(2):
            pool.dma(lambda e, h=h: e.dma_start(out=uTb[par][:, 8 * h:8 * h + 8, :], in_=uTv[:, 8 * h:8 * h + 8, eb * 256:(eb + 1) * 256]), d_u[par])
        pool.dma(lambda e: e.dma_start(out=vb[par], in_=vvv[eb]), d_v[par])
        return d_u[par].n, d_v[par].n

    def barrier():
        a, b, c = pe.n, act.n, dve.n
        pe.wait(act, b); pe.wait(dve, c)
        act.wait(pe, a); act.wait(dve, c)
        dve.wait(pe, a); dve.wait(act, b)

    for hf in range(2):
        tok0 = hf * 512
        barrier()
        wstate["n"] = 0
        wstate["pe_done"] = {}
        pool.wait(pe, pe.n)
        blocks = [load_wblock(wsrc(wq, 0)), load_wblock(wsrc(wq, 1))]
        pe.wait(act, last_copy)
        pe.wait(d_misc, MISC)
        qn = 0
        sc_copy = {}
        dve_q = {}
        act_sc = {}
        scn = 0
        for nb in range(4):
            n, par, dval = blocks[nb]
            pe.wait(d_w[par], dval)
            for c4 in range(4):
                ct = nb * 4 + c4
                h, side = ct // 2, ct % 2
                qb = qn % 2
                if qn >= 2:
                    pe.wait(dve, dve_q[qn - 2])
                for ch in range(NCH):
                    r = pe.op(lambda e, ch=ch, c4=c4, par=par, qb=qb: e.matmul(ps[:, qb, :], lhsT=Wv[par][:, ch, c4 * 128:(c4 + 1) * 128],
                                                                              rhs=XT[:, ch, tok0:tok0 + 512], start=(ch == 0), stop=(ch == NCH - 1)),
                              count=(ch == NCH - 1))
                dve.wait(pe, r)
                if qn >= 2:
                    dve.wait(pe, sc_pe[qn - 2])
                dve_q[qn] = dve.op(lambda e, qb=qb: e.tensor_copy(out=qTc[qb][:], in_=ps[:, qb, :]))
                pe.wait(dve, dve_q[qn])
                if qn >= 2:
                    pe.wait(act, act_sc[qn - 2])
                for tt in range(4):
                    r = pe.op(lambda e, tt=tt, qb=qb, side=side: e.matmul(ps[:, 2 + qb, tt * 128:(tt + 1) * 128], lhsT=qTc[qb][:, tt * 128:(tt + 1) * 128],
                                                                         rhs=keys_sb[:, side, :], start=True, stop=True), count=(tt == 3))
                if qn == 0:
                    sc_pe = {}
                sc_pe[qn] = r
                act.wait(pe, r)
                if side == 0:
                    act_sc[qn] = act.op(lambda e, h=h, qb=qb: e.copy(out=bav[:, :, h, :], in_=ps[:, 2 + qb, :].rearrange("p (t n) -> p t n", t=4)))
                else:
                    act_sc[qn] = act.op(lambda e, h=h, qb=qb: e.copy(out=sbv[:, :, h, :], in_=ps[:, 2 + qb, :].rearrange("p (t n) -> p t n", t=4)))
                qn += 1
            wstate["pe_done"][n] = pe.n
            if nb + 2 < 4:
                blocks.append(load_wblock(wsrc(wq, nb + 2)))
        scores_done_act = act.n
        scores_done_pe = pe.n
        B0 = peer_state["B"]
        pool.wait(pe, scores_done_pe)
        dma_vals = {}
        dma_vals[B0] = issue_peer_dma(B0) if NBLK > 0 else None

        dve.wait(act, scores_done_act)
        dve.wait(pool, ident_done)
        for tt in range(4):
            for h in range(8):
                sa = bav[:, tt, h, :]
                sbb_ = sbv[:, tt, h, :]
                for (src, off) in [(sa, 0), (sbb_, 24)]:
                    dve.op(lambda e, src=src, off=off: e.max(out=tk[:, off:off + 8], in_=src))
                    dve.op(lambda e, src=src, off=off: e.match_replace(out=swork[:], in_to_replace=tk[:, off:off + 8], in_values=src, imm_value=-1e30))
                    dve.op(lambda e, off=off: e.max(out=tk[:, off + 8:off + 16], in_=swork[:]))
                    dve.op(lambda e, off=off: e.match_replace(out=swork[:], in_to_replace=tk[:, off + 8:off + 16], in_values=swork[:], imm_value=-1e30))
                    dve.op(lambda e, off=off: e.max(out=tk[:, off + 16:off + 24], in_=swork[:]))
                dve.op(lambda e: e.tensor_tensor(out=cand, in0=tk[:, 0:22].unsqueeze(2).broadcast_to([128, 22, 22]),
                                                 in1=tk[:, 24:46].unsqueeze(1).broadcast_to([128, 22, 22]), op=ALU.add))
                cf = gelT[0][:, 0:484]
                dve.op(lambda e: e.max(out=tk[:, 48:56], in_=cf))
                dve.op(lambda e: e.match_replace(out=cwork, in_to_replace=tk[:, 48:56], in_values=cf, imm_value=-1e30))
                dve.op(lambda e: e.max(out=tk[:, 56:64], in_=cwork))
                dve.op(lambda e: e.match_replace(out=cwork, in_to_replace=tk[:, 56:64], in_values=cwork, imm_value=-1e30))
                dve.op(lambda e: e.max(out=tk[:, 64:72], in_=cwork))
                hs = hst[:, tt, h, :]
                dve.op(lambda e, hs=hs: e.tensor_scalar(out=hs[:, 0:1], in0=tk[:, 48:49], scalar1=-1.0, scalar2=None, op0=ALU.mult))
                i0 = dve.n
                act.wait(dve, i0)
                act.op(lambda e, hs=hs: e.activation(out=ejunk[:], in_=tk[:, 48:64], func=AF.Exp, bias=hs[:, 0:1], scale=1.0,
                                                     accum_out=hs[:, 1:2]))
                act.op(lambda e, hs=hs: e.activation(out=hs[:, 2:3], in_=hs[:, 1:2], func=AF.Ln))
                i1 = act.n
                dve.wait(act, i1)
                dve.op(lambda e, hs=hs: e.tensor_tensor(out=hs[:, 4:5], in0=hs[:, 0:1], in1=hs[:, 2:3], op=ALU.subtract))
                dve.op(lambda e, hs=hs: e.tensor_tensor(out=hs[:, 5:6], in0=tk[:, 63:64], in1=tk[:, 64:65], op=ALU.add))
                dve.op(lambda e, hs=hs: e.tensor_scalar(out=hs[:, 7:8], in0=hs[:, 5:6], scalar1=-0.5, scalar2=None, op0=ALU.mult))
                dve.op(lambda e, sa=sa, hs=hs: e.tensor_scalar(out=sa, in0=sa, scalar1=hs[:, 7:8], scalar2=None, op0=ALU.add))
                i2 = dve.n
                act.wait(dve, i2)
                act.op(lambda e, hs=hs: e.activation(out=hs[:, 6:7], in_=hs[:, 5:6], func=AF.Exp, bias=hs[:, 4:5], scale=0.5))
                dve.wait(act, act.n)
                dve.op(lambda e, tt=tt, h=h, hs=hs: e.tensor_scalar(out=diag[:, tt, h, :], in0=identf[:], scalar1=hs[:, 6:7], scalar2=None, op0=ALU.mult))
        topk_done_dve = dve.n
        topk_done_act = act.n
        if stop_after == "topk":
            dbgU = P.dram("dbgU", [128, 8192], F32, "ExternalOutput")
            dbgH = P.dram("dbgH", [128, 256], F32, "ExternalOutput")
            sp.wait(dve, topk_done_dve)
            sp.wait(act, topk_done_act)
            sp.dma(lambda e: e.dma_start(out=dbgU.ap(), in_=U[:]), d_o)
            sp.dma(lambda e: e.dma_start(out=dbgH.ap(), in_=hst[:].rearrange("p a b c -> p (a b c)")), d_o)
            return finish_out(P, X1, outB, d_o, (dve, topk_done_dve))

        pool.wait(dve, topk_done_dve)
        pool.wait(act, topk_done_act)
        for ebi in range(NBLK):
            B = peer_state["B"]
            par = B % 2
            uval, vval = dma_vals[B]
            if ebi + 1 < NBLK:
                dma_vals[B + 1] = issue_peer_dma(B + 1)
            pe.wait(d_u[par], uval)
            pe.wait(d_v[par], vval)
            pe_ht = {}
            for k in range(2):
                if B >= 1:
                    pe.wait(act, peer_state["act_gel"][(B - 1, k)])
                for ch in range(NCH):
                    r = pe.op(lambda e, ch=ch, k=k: e.matmul(ps[:, k, :], lhsT=uTb[par][:, ch, k * 128:(k + 1) * 128], rhs=XT[:, ch, tok0:tok0 + 512],
                                                             start=(ch == 0), stop=(ch == NCH - 1)), count=(ch == NCH - 1))
                pe_ht[k] = r
            for k in range(2):
                act.wait(pe, pe_ht[k])
                if B >= 1:
                    act.wait(dve, peer_state["dve_wt"][(B - 1, k)])
                peer_state["act_gel"][(B, k)] = act.op(lambda e, k=k: e.activation(out=gelT[k][:], in_=ps[:, k, :], func=GELU))
            for k in range(2):
                i = 2 * (B % 64) + k
                if B >= 1:
                    pe.wait(dve, peer_state["dve_wt"][(B - 1, k)])
                for tt in range(4):
                    r = peer_state["r"]
                    rb = r % 2
                    if r >= 2:
                        pool.wait(dve, peer_state["dve_g"][r - 2])
                    peer_state["pool_s"][r] = pool.op(lambda e, rb=rb, tt=tt, i=i: e.tensor_tensor(
                        out=sE[rb][:], in0=sbv[:, tt, :, :], in1=bav[:, tt, :, i:i + 1].broadcast_to([128, 8, 128]), op=ALU.add), nosync=True)
                    act.wait(pool, peer_state["pool_s"][r])
                    peer_state["act_e"][r] = act.op(lambda e, rb=rb: e.activation(out=sE[rb][:], in_=sE[rb][:], func=AF.Exp), nosync=True)
                    dve.wait(act, peer_state["act_e"][r])
                    if r >= 2:
                        dve.wait(pe, peer_state["pe_gt"][r - 2])
                    peer_state["dve_g"][r] = dve.op(lambda e, rb=rb: e.scalar_tensor_tensor(
                        out=gE[rb][:], in0=sE[rb][:], scalar=1.0, in1=sE[rb][:], op0=ALU.is_ge, op1=ALU.mult), nosync=True)
                    pe.wait(dve, peer_state["dve_g"][r])
                    for h in range(8):
                        rr = pe.op(lambda e, rb=rb, tt=tt, h=h, k=k: e.matmul(
                            ps[:, 2 + k, tt * 128:(tt + 1) * 128], lhsT=gE[rb][:, h, :], rhs=diag[:, tt, h, :], start=(h == 0), stop=(h == 7)),
                            count=(h == 7))
                    peer_state["pe_gt"][r] = rr
                    peer_state["r"] += 1
                gt_done = pe.n
                dve.wait(pe, gt_done)
                dve.wait(act, peer_state["act_gel"][(B, k)])
                if B >= 2:
                    dve.wait(pe, peer_state["pe_blk"][B - 2])
                peer_state["dve_wt"][(B, k)] = dve.op(lambda e, k=k: e.tensor_tensor(out=WT[par][k][:], in0=ps[:, 2 + k, :], in1=gelT[k][:], op=ALU.mult))
            pe.wait(dve, peer_state["dve_wt"][(B, 0)])
            pe.wait(dve, peer_state["dve_wt"][(B, 1)])
            for tt in range(4):
                gtt = hf * 4 + tt
                for nb in range(4):
                    pe.wait(dve, peer_state["dve_accb"][nb])
                    for k in range(2):
                        r = pe.op(lambda e, tt=tt, nb=nb, k=k: e.matmul(ps[:, 4 + nb, :], lhsT=WT[par][k][:, tt * 128:(tt + 1) * 128],
                                                                      rhs=vb[par][:, k, nb * 512:(nb + 1) * 512], start=(k == 0), stop=(k == 1)),
                                  count=(k == 1))
                    dve.wait(pe, r)
                    peer_state["dve_accb"][nb] = dve.op(lambda e, gtt=gtt, nb=nb: e.tensor_tensor(
                        out=X1[:, gtt, nb * 512:(nb + 1) * 512], in0=X1[:, gtt, nb * 512:(nb + 1) * 512], in1=ps[:, 4 + nb, :], op=ALU.add), nosync=True)
            peer_state["pe_blk"][B] = pe.n
            peer_state["B"] += 1
        act.wait(dve, dve.n)
        act.wait(pe, pe.n)
    peer_done_dve = dve.n
    if stop_after == "dbgwt":
        dbgW = P.dram("dbgW", [2, 128, 512], BF16, "ExternalOutput")
        dbgG = P.dram("dbgG", [2, 128, 512], F32, "ExternalOutput")
        sp.wait(dve, peer_done_dve)
        sp.wait(act, act.n)
        for k in range(2):
            sp.dma(lambda e, k=k: e.dma_start(out=dbgW.ap()[k], in_=WT[0][k][:]), d_o)
            sp.dma(lambda e, k=k: e.dma_start(out=dbgG.ap()[k], in_=gelT[k][:]), d_o)
        return finish_out(P, X1, outB, d_o, (dve, peer_done_dve))
    if stop_after == "peer":
        return finish_out(P, X1, outB, d_o, (dve, peer_done_dve))

    sp.wait(dve, peer_done_dve)
    sp.wait(act, act.n)
    ln2 = load_ln(1)
    pe.wait(dve, peer_done_dve)
    for tt in range(TT):
        i = layer_norm(tt, ln2)
        last_copy = transpose_tile(lambda ch, tt=tt: X1[:, tt, ch * 128:(ch + 1) * 128], tt, (dve, i))
    ln2_done = dve.n
    d_pp = P.dsem("bpp")
    sp.wait(dve, ln2_done)
    sp.dma(lambda e: e.dma_start(out=pTf, in_=pT.ap().rearrange("(k p) t -> p k t", p=128)), d_pp)
    for k in range(2):
        sp.dma(lambda e, k=k: e.dma_start(out=projf[k], in_=wproj.ap()[k * 128:(k + 1) * 128, :]), d_pp)
    wstate["n"] = 0
    wstate["pe_done"] = {}
    pool.wait(pe, pe.n)
    blocks = [load_wblock(wsrc(wgate, 0)), load_wblock(wsrc(wgate, 1))]
    pe.wait(act, last_copy)
    pe.wait(d_pp, 48)
    cnt = 0
    dve_y = {}
    act_sig = {}
    for nb in range(4):
        n, par, dval = blocks[nb]
        pe.wait(d_w[par], dval)
        for tt in range(TT):
            b0 = (cnt % 2) * 2
            if cnt >= 2:
                pe.wait(dve, dve_y[cnt - 2])
            for ch in range(NCH):
                pe.op(lambda e, ch=ch, tt=tt, par=par, b0=b0: e.matmul(ps[:, b0, :], lhsT=XT[:, ch, tt * 128:(tt + 1) * 128], rhs=Wv[par][:, ch, :],
                                                                      start=(ch == 0), stop=(ch == NCH - 1)), count=False)
            for k in range(2):
                r = pe.op(lambda e, k=k, tt=tt, nb=nb, b0=b0: e.matmul(ps[:, b0 + 1, :], lhsT=pTf[:, k, tt * 128:(tt + 1) * 128],
                                                                      rhs=projf[k][:, nb * 512:(nb + 1) * 512], start=(k == 0), stop=(k == 1)), count=(k == 1))
            act.wait(pe, r)
            if cnt >= 1:
                act.wait(dve, dve_y[cnt - 1])
            act_sig[cnt] = act.op(lambda e, b0=b0: e.activation(out=sig, in_=ps[:, b0, :], func=AF.Sigmoid))
            dve.wait(act, act_sig[cnt])
            dve.op(lambda e, b0=b0: e.tensor_tensor(out=tmpb, in0=sig, in1=ps[:, b0 + 1, :], op=ALU.mult))
            dve_y[cnt] = dve.op(lambda e, tt=tt, nb=nb: e.scalar_tensor_tensor(out=X1[:, tt, nb * 512:(nb + 1) * 512], in0=X1[:, tt, nb * 512:(nb + 1) * 512],
                                                                              scalar=ALPHA, in1=tmpb, op0=ALU.mult, op1=ALU.add))
            cnt += 1
        wstate["pe_done"][n] = pe.n
        if nb + 2 < 4:
            blocks.append(load_wblock(wsrc(wgate, nb + 2)))
    ple_done = dve.n
    sp.wait(dve, ple_done)
    ln3 = load_ln(2)
    for tt in range(TT):
        i = layer_norm(tt, ln3)
        sp.wait(dve, i)
        sp.dma(lambda e, tt=tt: e.dma_start(out=outB.ap()[tt * 128:(tt + 1) * 128, :], in_=X1[:, tt, :]), d_o)
    sp.wait(d_o, d_o.n)
    return None


def finish_out(P, X1, outB, d_o, after):
    sp = P.sp
    sp.wait(after[0], after[1])
    for tt in range(TT):
        sp.dma(lambda e, tt=tt: e.dma_start(out=outB.ap()[tt * 128:(tt + 1) * 128, :], in_=X1[:, tt, :]), d_o)
    sp.wait(d_o, d_o.n)
    return None


I32 = mybir.dt.int32
_CACHE = {}


def build_fused():
    P = Prog()
    main_stack = P.stack
    attn_loc_t = P.dram("attn_loc", [512, 4096], BF16, "Internal")
    attn_all_t = P.dram("attn_all", [4096, 4096], BF16, "Internal")
    attn_loc_v = attn_loc_t.ap().rearrange("a (b f) -> (a b) f", f=256)
    attn_all_v = attn_all_t.ap().rearrange("a (b f) -> (a b) f", f=256)
    aidx = P.dram("aidx", [128, 64], I32, "ExternalInput")
    cc = P.dsem("cc")
    P.stack = contextlib.ExitStack()
    build_phase_a(P, attn_loc_v)
    P.stack.close()
    for d in P.dsems:
        if d.name in ("o0", "o1"):
            P.pool.wait(d, d.n)
    rg = [list(range(8))]
    P.pool.cc(lambda e: e.collective_compute("AllGather", ALU.bypass, replica_groups=rg,
                                             ins=[attn_loc_t.ap()], outs=[attn_all_t.ap()]), cc)
    P.stack = contextlib.ExitStack()
    build_phase_b(P, attn_all_v, wait_attn=(cc, 1), gather=aidx.ap())
    sub = P.stack
    P.stack = main_stack
    nc = P.finish()
    sub.close()
    return nc


def _rope_tables():
    pos = np.arange(S, dtype=np.float32)
    inv = (np.float32(500000.0) ** (-np.arange(0, 16, 2, dtype=np.float32) / np.float32(16))).astype(np.float32)
    ang = (pos[:, None] * inv[None, :]).astype(np.float32)
    cos = np.cos(ang).astype(np.float32).T
    sin = np.sin(ang).astype(np.float32).T
    C = np.ones((128, S), np.float32)
    Sn = np.zeros((128, S), np.float32)
    for base in (0, 64):
        C[base:base + 8] = cos
        C[base + 8:base + 16] = cos
        Sn[base:base + 8] = -sin
        Sn[base + 8:base + 16] = sin
    return C, Sn


def host_prep(inp):
    x = np.asarray(inp["x"], np.float32)[0]
    w_in = np.asarray(inp["w_in"], np.float32)[0]
    w_out = np.asarray(inp["w_out"], np.float32)[0]
    xT = np.ascontiguousarray(x.T)
    C, Sn = _rope_tables()
    perm = np.arange(128)
    for base in (0, 64):
        perm[base:base + 8] = np.arange(base + 8, base + 16)
        perm[base + 8:base + 16] = np.arange(base, base + 8)
    lamv = np.concatenate([np.asarray(inp[k], np.float32)[0] for k in ("lambda_q1", "lambda_k1", "lambda_q2", "lambda_k2")])[None, :]
    lamv = np.ascontiguousarray(lamv, dtype=np.float32)
    gsub = np.asarray(inp["diff_subln_g"], np.float32).reshape(1, 128)
    rows = []
    for s in range(8):
        rows.append(w_out[s * 128:(s + 1) * 128])
        rows.append(w_out[1024 + s * 128:1024 + (s + 1) * 128])
    woutP = np.ascontiguousarray(np.concatenate(rows, axis=0))
    lng = np.ascontiguousarray(np.asarray(inp["ln_g"], np.float32)[0])
    lnb = np.ascontiguousarray(np.asarray(inp["ln_b"], np.float32)[0])
    wq = np.ascontiguousarray(np.asarray(inp["peer_wq"], np.float32)[0])
    keysT = np.ascontiguousarray(np.transpose(np.asarray(inp["peer_keys"], np.float32)[0], (0, 2, 1)))
    uT = np.ascontiguousarray(np.asarray(inp["peer_u"], np.float32)[0].T)
    vv = np.ascontiguousarray(np.asarray(inp["peer_v"], np.float32)[0])
    wgate = np.ascontiguousarray(np.asarray(inp["ple_gate"], np.float32)[0])
    wproj = np.ascontiguousarray(np.asarray(inp["ple_proj"], np.float32)[0])
    p = np.asarray(inp["p"], np.float32)[0, 0]
    maps = []
    for c in range(8):
        sl = lambda off: w_in[:, off + c * 128: off + (c + 1) * 128]
        dq, dk, dv, sq, sk, sv = sl(0), sl(1024), sl(2048), sl(3072), sl(4096), sl(5120)
        wA = np.ascontiguousarray(np.concatenate([dq, dk, sq, sk, dq[:, perm], dk[:, perm], dv, sv], axis=1))
        tok = slice(c * 1024, (c + 1) * 1024)
        aidx = np.zeros((128, 64), np.int32)
        for tt in range(8):
            for s in range(8):
                aidx[:, tt * 8 + s] = s * S + c * 1024 + tt * 128 + np.arange(128)
        maps.append({"xT": xT, "wA": wA, "ropeC": C, "ropeS": Sn, "lamv": lamv, "gsub": gsub,
                     "aidx": aidx, "xs": np.ascontiguousarray(x[tok]), "woutP": woutP, "lng": lng, "lnb": lnb,
                     "wq": wq, "keysT": keysT, "uT": uT, "vv": vv, "wgate": wgate, "wproj": wproj,
                     "pT": np.ascontiguousarray(p[tok].T)})
    return maps


def kernel(**inputs):
    if "nc" not in _CACHE:
        _CACHE["nc"] = build_fused()
    nc = _CACHE["nc"]
    maps = host_prep(inputs)
    res = run_bass_kernel_spmd(nc, maps, core_ids=list(range(8)))
    out = np.concatenate([np.asarray(res.results[c]["outB"], dtype=np.float32) for c in range(8)], axis=0)
    return out.reshape(1, S, D)
```

```python
import contextlib
import types
import numpy as np
import concourse.bass as bass
import concourse.mybir as mybir
from concourse.bass_utils import run_bass_kernel_spmd

F32 = mybir.dt.float32
BF16 = mybir.dt.bfloat16
AF = mybir.ActivationFunctionType
ALU = mybir.AluOpType
AX = mybir.AxisListType


def _freeze(fn):
    if getattr(fn, "__closure__", None) is None:
        return fn
    cells = []
    for c in fn.__closure__:
        try:
            cells.append(types.CellType(c.cell_contents))
        except ValueError:
            cells.append(c)
    g = types.FunctionType(fn.__code__, fn.__globals__, fn.__name__, fn.__defaults__, tuple(cells))
    g.__kwdefaults__ = fn.__kwdefaults__
    return g


class Eng:
    def __init__(self, name, serial=True):
        self.name = name
        self.ops = []
        self.n = 0
        self.sem = None
        self.waited = {}
        self.serial = serial

    def wait(self, other, val):
        if val <= 0:
            return
        key = id(other)
        if self.waited.get(key, 0) >= val:
            return
        self.waited[key] = val
        self.ops.append(("wait", other, val))

    def op(self, fn, count=True, nosync=False):
        if count and self.serial and not nosync:
            self.wait(self, self.n)
        self.ops.append(("op", _freeze(fn), count))
        if count:
            self.n += 1
        return self.n

    def dma(self, fn, dsem):
        self.ops.append(("dma", _freeze(fn), dsem))
        dsem.n += 16
        return dsem.n

    def cc(self, fn, dsem):
        self.ops.append(("cc", _freeze(fn), dsem))
        dsem.n += 1
        return dsem.n

    def replay(self, e):
        for o in self.ops:
            if o[0] == "cc":
                ins = o[1](e)
                ins.then_inc(o[2].sem, 1)
            elif o[0] == "wait":
                e.wait_ge(o[1].sem, o[2])
            elif o[0] == "op":
                ins = o[1](e)
                if o[2]:
                    ins.then_inc(self.sem, 1)
            else:
                ins = o[1](e)
                ins.then_inc(o[2].sem, 16)


class DmaSem:
    def __init__(self, name):
        self.name = name
        self.n = 0
        self.sem = None


class Prog:
    def __init__(self):
        self.nc = bass.Bass("TRN2", target_bir_lowering=False)
        self.pe = Eng("pe", serial=False)
        self.act = Eng("act")
        self.dve = Eng("dve")
        self.pool = Eng("pool")
        self.sp = Eng("sp", serial=False)
        self.engs = [self.pe, self.act, self.dve, self.pool, self.sp]
        self.dsems = []
        self.stack = contextlib.ExitStack()

    def dsem(self, name):
        d = DmaSem(name)
        self.dsems.append(d)
        return d

    def sb(self, name, shape, dt):
        return self.stack.enter_context(self.nc.sbuf_tensor(name, list(shape), dt))

    def ps(self, name, shape, dt=F32):
        return self.stack.enter_context(self.nc.psum_tensor(name, list(shape), dt))

    def dram(self, name, shape, dt, kind):
        return self.nc.dram_tensor(name, list(shape), dt, kind=kind)

    def finish(self):
        nc = self.nc
        for e in self.engs:
            e.sem = self.stack.enter_context(nc.semaphore("sem_" + e.name))
        for d in self.dsems:
            d.sem = self.stack.enter_context(nc.semaphore("ds_" + d.name))
        with nc.Block() as block:
            @block.tensor
            def _(t):
                self.pe.replay(t)

            @block.scalar
            def _(t):
                self.act.replay(t)

            @block.vector
            def _(t):
                self.dve.replay(t)

            @block.gpsimd
            def _(t):
                self.pool.replay(t)

            @block.sync
            def _(t):
                self.sp.replay(t)
        self.stack.close()
        return nc


S = 8192
NT = S // 128
NB = S // 512
D = 2048
NCH = D // 128


def build_phase_a(P, out_attn, NQB=NB):
    nc = P.nc
    pe, act, dve, pool, sp = P.pe, P.act, P.dve, P.pool, P.sp
    xT = P.dram("xT", [D, S], F32, "ExternalInput")
    wA = P.dram("wA", [D, 1024], F32, "ExternalInput")
    ropeC = P.dram("ropeC", [128, S], F32, "ExternalInput")
    ropeS = P.dram("ropeS", [128, S], F32, "ExternalInput")
    lamv = P.dram("lamv", [1, 256], F32, "ExternalInput")
    gsub = P.dram("gsub", [1, 128], F32, "ExternalInput")

    xTv = xT.ap().rearrange("(c p) t -> p c t", p=128)
    wAv = wA.ap().rearrange("(c p) n -> p c n", p=128)

    wsb = P.sb("wsb", [128, NCH, 1024], BF16)
    xb = [P.sb(f"xb{i}", [128, NCH, 512], BF16) for i in range(2)]
    cb = [P.sb(f"cb{i}", [128, 512], F32) for i in range(2)]
    sbb = [P.sb(f"sbb{i}", [128, 512], F32) for i in range(2)]
    QT = P.sb("QT", [128, S], BF16)
    KT = P.sb("KT", [128, S], BF16)
    QsT = P.sb("QsT", [128, S], BF16)
    KsT = P.sb("KsT", [128, S], BF16)
    Vd = P.sb("Vd", [128, NT, 130], BF16)
    Vs = P.sb("Vs", [128, NT, 128], BF16)
    t1 = P.sb("t1", [128, 512], F32)
    t2 = P.sb("t2", [128, 512], F32)
    cst = P.sb("cst", [128, 4, 128], BF16)
    cstf = P.sb("cstf", [128, 128], F32)
    lamt = P.sb("lamt", [128, 256], F32)
    lamw = P.sb("lamw", [128, 128], F32)
    lams = P.sb("lams", [128, 8], F32)
    g08 = P.sb("g08", [128, 128], F32)
    PT1 = [xb[0][:, 0 + i, :] for i in range(2)]
    PT2 = [xb[0][:, 2 + i, :] for i in range(2)]
    fa = P.sb("fa", [128, 128], F32)
    fod = P.sb("fod", [128, 128], F32)
    fjunk = P.sb("fjunk", [128, 128], F32)
    fst = P.sb("fst", [128, 8], F32)
    oout = [P.sb(f"oout{i}", [128, 128], BF16) for i in range(2)]
    eb = [cb[0], cb[1]]
    spb = [sbb[0], sbb[1]]
    nkb = [xb[0][:, 4 + i, :] for i in range(2)]
    Sacc = P.sb("Sacc", [128, 512], F32)
    Sbf = [xb[0][:, 6 + i, :] for i in range(2)]
    argb = [t1, t2]
    Ab = [xb[0][:, 8 + i, :] for i in range(2)]

    ps = P.ps("psA", [128, 8, 512], F32)

    d_w = P.dsem("w")
    d_x = [P.dsem("x0"), P.dsem("x1")]
    d_cs = [P.dsem("cs0"), P.dsem("cs1")]
    d_misc = P.dsem("misc")
    d_out = [P.dsem("o0"), P.dsem("o1")]

    sp.dma(lambda e: e.dma_start(out=lamt[:], in_=lamv.ap().partition_broadcast(128)), d_misc)
    sp.dma(lambda e: e.dma_start(out=g08[:], in_=gsub.ap().partition_broadcast(128)), d_misc)
    def mk(idx, pattern_step, cm, cmp, base=0):
        pool.op(lambda e: e.memset(cstf[:], 1.0))
        if cmp is not None:
            pool.op(lambda e: e.affine_select(out=cstf[:], in_=cstf[:], pattern=[[pattern_step, 128]],
                                              compare_op=cmp, fill=0.0, base=base, channel_multiplier=cm))
        pool.op(lambda e: e.tensor_copy(out=cst[:, idx, :], in_=cstf[:]))
    mk(0, 1, -1, ALU.is_ge)
    mk(1, 1, -1, ALU.is_gt)
    mk(2, -1, 1, ALU.is_gt)
    mk(3, 0, 0, None)
    pool.op(lambda e: e.memset(Vd[:, :, 128:130], 1.0))
    cst_done = pool.n
    maskLE = cst[:, 0, :]
    maskLT = cst[:, 1, :]
    triGT = cst[:, 2, :]
    ones = cst[:, 3, :]

    dve.wait(d_misc, 32)
    dve.op(lambda e: e.tensor_tensor(out=lamw[:, 0:64], in0=lamt[:, 0:64], in1=lamt[:, 64:128], op=ALU.mult))
    dve.op(lambda e: e.tensor_tensor(out=lamw[:, 64:128], in0=lamt[:, 128:192], in1=lamt[:, 192:256], op=ALU.mult))
    dve.op(lambda e: e.reduce_sum(out=lams[:, 0:2], in_=lamw[:].rearrange("p (a b) -> p a b", a=2), axis=AX.X))
    i_ls = dve.n
    act.wait(dve, i_ls)
    act.op(lambda e: e.activation(out=lams[:, 2:4], in_=lams[:, 0:2], func=AF.Exp))
    i_le = act.n
    dve.wait(act, i_le)
    dve.op(lambda e: e.scalar_tensor_tensor(out=lams[:, 4:5], in0=lams[:, 3:4], scalar=-0.2, in1=lams[:, 2:3],
                                            op0=ALU.add, op1=ALU.subtract))
    dve.op(lambda e: e.tensor_scalar(out=g08[:], in0=g08[:], scalar1=0.8, scalar2=None, op0=ALU.mult))
    neglam = lams[:, 4:5]

    for c4 in range(4):
        pool.dma(lambda e, c4=c4: e.dma_start(out=wsb[:, 4 * c4:4 * c4 + 4, :], in_=wAv[:, 4 * c4:4 * c4 + 4, :]), d_w)
    pe_blk_done = [0] * NB
    pe_g1 = [0] * NB
    pe_g2 = [0] * NB
    dve_g1 = [0] * NB
    act_g2 = [0] * NB
    for b in range(NB):
        bi = b % 2
        tok = slice(b * 512, (b + 1) * 512)
        if b >= 2:
            pool.wait(pe, pe_blk_done[b - 2])
            sp.wait(dve, dve_g1[b - 2])
        for h in range(2):
            pool.dma(lambda e, bi=bi, h=h, tok=tok: e.dma_start(out=xb[bi][:, 8 * h:8 * h + 8, :], in_=xTv[:, 8 * h:8 * h + 8, tok]), d_x[bi])
        sp.dma(lambda e, bi=bi, tok=tok: e.dma_start(out=cb[bi][:], in_=ropeC.ap()[:, tok]), d_cs[bi])
        sp.dma(lambda e, bi=bi, tok=tok: e.dma_start(out=sbb[bi][:], in_=ropeS.ap()[:, tok]), d_cs[bi])
        xval = d_x[bi].n
        csval = d_cs[bi].n
        pe.wait(d_w, 64)
        pe.wait(d_x[bi], xval)
        if b >= 1:
            pe.wait(dve, dve_g1[b - 1])
        for g, col in enumerate([0, 512, 128, 640]):
            for ch in range(NCH):
                fn = lambda e, g=g, col=col, ch=ch, bi=bi: e.matmul(ps[:, g, :], lhsT=wsb[:, ch, col:col + 128], rhs=xb[bi][:, ch, :],
                                                                    start=(ch == 0), stop=(ch == NCH - 1))
                last = (g == 3 and ch == NCH - 1)
                r = pe.op(fn, count=last)
        pe_g1[b] = pe.n
        if b >= 1:
            pe.wait(act, act_g2[b - 1])
        for g, col in [(4, 256), (5, 384)]:
            for ch in range(NCH):
                pe.op(lambda e, g=g, col=col, ch=ch, bi=bi: e.matmul(ps[:, g, :], lhsT=wsb[:, ch, col:col + 128], rhs=xb[bi][:, ch, :],
                                                                     start=(ch == 0), stop=(ch == NCH - 1)), count=False)
        for t in range(4):
            for ch in range(NCH):
                last = (t == 3 and ch == NCH - 1)
                pe.op(lambda e, t=t, ch=ch, bi=bi: e.matmul(ps[:, 6 + t // 2, (t % 2) * 256:(t % 2) * 256 + 256],
                                                            lhsT=xb[bi][:, ch, t * 128:(t + 1) * 128], rhs=wsb[:, ch, 768:1024],
                                                            start=(ch == 0), stop=(ch == NCH - 1)), count=last)
        pe_g2[b] = pe.n
        pe_blk_done[b] = pe.n
        dve.wait(pe, pe_g1[b])
        dve.wait(d_cs[bi], csval)
        for (bq, bs, dst) in [(0, 1, QT), (2, 3, KT)]:
            dve.op(lambda e, bq=bq, bi=bi: e.tensor_tensor(out=t1[:], in0=ps[:, bq, :], in1=cb[bi][:], op=ALU.mult))
            dve.op(lambda e, bs=bs, bi=bi: e.tensor_tensor(out=t2[:], in0=ps[:, bs, :], in1=sbb[bi][:], op=ALU.mult))
            dve.op(lambda e, dst=dst, tok=tok: e.tensor_tensor(out=dst[:, tok], in0=t1[:], in1=t2[:], op=ALU.add))
        dve_g1[b] = dve.n
        act.wait(pe, pe_g2[b])
        act.op(lambda e, tok=tok: e.copy(out=QsT[:, tok], in_=ps[:, 4, :]))
        act.op(lambda e, tok=tok: e.copy(out=KsT[:, tok], in_=ps[:, 5, :]))
        for h in range(2):
            src = ps[:, 6 + h, :].rearrange("p (t c) -> p t c", t=2)
            act.op(lambda e, src=src, h=h, b=b: e.copy(out=Vd[:, 4 * b + 2 * h:4 * b + 2 * h + 2, 0:128], in_=src[:, :, 0:128]))
            act.op(lambda e, src=src, h=h, b=b: e.copy(out=Vs[:, 4 * b + 2 * h:4 * b + 2 * h + 2, :], in_=src[:, :, 128:256]))
        act_g2[b] = act.n
    proj_dve = dve.n
    proj_act = act.n
    proj_pe = pe.n

    steps = []
    for qb in range(NQB):
        for j in range(4 * qb + 4):
            steps.append((qb, j))
    ns = len(steps)
    SC_D = 0.125
    act_exp = [0] * ns
    dve_mask = [0] * ns
    pe_qk = [0] * ns
    pe_pv = [0] * ns
    dve_fin = {}
    pe.wait(dve, proj_dve)
    pe.wait(act, proj_act)
    pe.wait(pool, cst_done)
    act.wait(pe, proj_pe)
    dve.wait(pe, proj_pe)
    dve.wait(pool, cst_done)
    n_out = 0

    def emit_qk(s):
        qb, j = steps[s]
        bi = s % 2
        c0 = max(0, j - 4 * qb) * 128
        q0 = qb * 512
        if s >= 2:
            pe.wait(act, act_exp[s - 2])
        pe.op(lambda e: e.matmul(ps[:, bi, c0:512], lhsT=KT[0:64, j * 128:(j + 1) * 128], rhs=QT[0:64, q0 + c0:q0 + 512],
                                 start=True, stop=True), count=False)
        pe_qk[s] = pe.op(lambda e: e.matmul(ps[:, 2 + bi, c0:512], lhsT=KT[64:128, j * 128:(j + 1) * 128], rhs=QT[64:128, q0 + c0:q0 + 512],
                                            start=True, stop=True))

    def emit_rest(s):
        nonlocal n_out
        qb, j = steps[s]
        bi = s % 2
        d = j - 4 * qb
        c0 = max(0, d) * 128
        act.wait(pe, pe_qk[s])
        if s >= 2:
            act.wait(pe, pe_pv[s - 2])
            if dve_mask[s - 2]:
                act.wait(dve, dve_mask[s - 2])
        act.op(lambda e: e.activation(out=PT1[bi][:, c0:512], in_=ps[:, bi, c0:512], func=AF.Exp, scale=SC_D), nosync=True)
        act_exp[s] = act.op(lambda e: e.activation(out=PT2[bi][:, c0:512], in_=ps[:, 2 + bi, c0:512], func=AF.Exp, scale=SC_D), nosync=True)
        if d >= 0:
            dve.wait(act, act_exp[s])
            dve.op(lambda e: e.tensor_tensor(out=PT1[bi][:, c0:c0 + 128], in0=PT1[bi][:, c0:c0 + 128], in1=maskLE, op=ALU.mult), nosync=True)
            dve_mask[s] = dve.op(lambda e: e.tensor_tensor(out=PT2[bi][:, c0:c0 + 128], in0=PT2[bi][:, c0:c0 + 128], in1=maskLE, op=ALU.mult), nosync=True)
        pe.wait(act, act_exp[s])
        if d >= 0:
            pe.wait(dve, dve_mask[s])
        if j == 0 and qb >= 1:
            for t in range(4):
                pe.wait(dve, dve_fin[(qb - 1, t)])
        tl = list(range(max(0, d), 4))
        for t in tl:
            lastj = (j == 4 * qb + t)
            pe.op(lambda e, t=t, lastj=lastj: e.matmul(ps[:, 4 + t, 0:129], lhsT=PT1[bi][:, t * 128:(t + 1) * 128], rhs=Vd[:, j, 0:129],
                                                       start=(j == 0), stop=lastj, skip_group_check=True), count=False)
            cnt = (t == tl[-1])
            r = pe.op(lambda e, t=t, lastj=lastj: e.matmul(ps[:, 4 + t, 256:385], lhsT=PT2[bi][:, t * 128:(t + 1) * 128], rhs=Vd[:, j, 0:129],
                                                           start=False, stop=lastj, skip_group_check=True), count=cnt)
        pe_pv[s] = pe.n
        if d >= 0:
            t = d
            qt = 4 * qb + t
            O1 = ps[:, 4 + t, 0:128]
            O2 = ps[:, 4 + t, 256:384]
            dve.wait(pe, pe_pv[s])
            dve.op(lambda e: e.reciprocal(out=fst[:, 0:1], in_=ps[:, 4 + t, 128:129]))
            dve.op(lambda e: e.reciprocal(out=fst[:, 1:2], in_=ps[:, 4 + t, 384:385]))
            dve.op(lambda e: e.tensor_tensor(out=fst[:, 2:3], in0=fst[:, 1:2], in1=neglam, op=ALU.mult))
            dve.op(lambda e: e.tensor_scalar(out=fa[:], in0=O1, scalar1=fst[:, 0:1], scalar2=None, op0=ALU.mult))
            dve_fin[(qb, t)] = dve.op(lambda e: e.scalar_tensor_tensor(out=fod[:], in0=O2, scalar=fst[:, 2:3], in1=fa[:], op0=ALU.mult, op1=ALU.add))
            i1 = dve.n
            act.wait(dve, i1)
            act.op(lambda e: e.activation(out=fjunk[:], in_=fod[:], func=AF.Square, accum_out=fst[:, 3:4]))
            act.op(lambda e: e.activation(out=fst[:, 4:5], in_=fst[:, 3:4], func=AF.Ln, bias=1e-5, scale=1.0 / 128))
            act.op(lambda e: e.activation(out=fst[:, 5:6], in_=fst[:, 4:5], func=AF.Exp, scale=-0.5))
            i2 = act.n
            dve.wait(act, i2)
            ob = n_out % 2
            if n_out >= 2:
                dve.wait(d_out[ob], 16 * (n_out // 2))
            dve.op(lambda e, ob=ob: e.scalar_tensor_tensor(out=oout[ob][:], in0=fod[:], scalar=fst[:, 5:6], in1=g08[:], op0=ALU.mult, op1=ALU.mult))
            i3 = dve.n
            sp.wait(dve, i3)
            sp.dma(lambda e, ob=ob, qt=qt: e.dma_start(out=out_attn[qt * 128:(qt + 1) * 128, 0:128], in_=oout[ob][:]), d_out[ob])
            n_out += 1

    emit_qk(0)
    for s in range(ns):
        if s + 1 < ns:
            emit_qk(s + 1)
        emit_rest(s)
    diff_pe = pe.n
    diff_act = act.n
    diff_dve = dve.n

    SC_S = 128 ** -0.5
    steps = []
    for qb in range(NQB):
        for j in range(4 * qb + 3, -1, -1):
            steps.append((qb, j))
    ns = len(steps)
    pe_z = [0] * ns
    pe_c = [0] * ns
    pe_av = [0] * ns
    act_sp = [0] * ns
    act_A = [0] * ns
    dve_nk = [0] * ns
    dve_arg = [0] * ns
    dve_mA = [0] * ns
    pool_S = [0] * ns
    fin_sb = {}
    pe.wait(dve, diff_dve)
    pe.wait(act, diff_act)
    act.wait(pe, diff_pe)
    dve.wait(pe, diff_pe)

    def info(s):
        qb, j = steps[s]
        d = j - 4 * qb
        c0 = max(0, d) * 128
        first = (j == 4 * qb + 3)
        return qb, j, d, c0, first

    def sb_qk(s):
        qb, j, d, c0, first = info(s)
        bi = s % 2
        q0 = qb * 512
        if s >= 2:
            pe.wait(dve, dve_nk[s - 2])
        pe_z[s] = pe.op(lambda e: e.matmul(ps[:, bi, c0:512], lhsT=KsT[:, j * 128:(j + 1) * 128], rhs=QsT[:, q0 + c0:q0 + 512],
                                           start=True, stop=True))

    def sb_front(s):
        qb, j, d, c0, first = info(s)
        bi = s % 2
        act.wait(pe, pe_z[s])
        if s >= 2:
            act.wait(dve, dve_arg[s - 2])
        act.op(lambda e: e.activation(out=eb[bi][:, c0:512], in_=ps[:, bi, c0:512], func=AF.Exp, scale=-SC_S), nosync=True)
        act_sp[s] = act.op(lambda e: e.activation(out=spb[bi][:, c0:512], in_=eb[bi][:, c0:512], func=AF.Ln, bias=1.0, scale=1.0))
        dve.wait(act, act_sp[s])
        if s >= 2:
            dve.wait(pe, pe_c[s - 2])
            dve.wait(pool, pool_S[s - 2])
        dve.op(lambda e: e.scalar_tensor_tensor(out=nkb[bi][:, c0:512], in0=ps[:, bi, c0:512], scalar=SC_S, in1=spb[bi][:, c0:512],
                                                op0=ALU.mult, op1=ALU.add), nosync=True)
        if d >= 0:
            dve.op(lambda e: e.tensor_tensor(out=nkb[bi][:, c0:c0 + 128], in0=nkb[bi][:, c0:c0 + 128], in1=maskLT, op=ALU.mult))
        dve_nk[s] = dve.n
        pool.wait(dve, dve_nk[s])
        if first:
            pool.op(lambda e: e.memset(Sacc[:], 0.0))
        pool.op(lambda e: e.tensor_tensor(out=Sacc[:, c0:512], in0=Sacc[:, c0:512], in1=nkb[bi][:, c0:512], op=ALU.add))
        if s >= 1:
            pool.wait(pe, pe_c[s - 1])
        pool_S[s] = pool.op(lambda e: e.tensor_copy(out=Sbf[bi][:], in_=Sacc[:]))

    def sb_cum(s):
        qb, j, d, c0, first = info(s)
        bi = s % 2
        pe.wait(dve, dve_nk[s])
        if s >= 2:
            pe.wait(dve, dve_arg[s - 2])
        if first:
            pe_c[s] = pe.op(lambda e: e.matmul(ps[:, 2 + bi, c0:512], lhsT=triGT, rhs=nkb[bi][:, c0:512], start=True, stop=True))
        else:
            pe.op(lambda e: e.matmul(ps[:, 2 + bi, c0:512], lhsT=triGT, rhs=nkb[bi][:, c0:512], start=True, stop=False), count=False)
            pe.wait(pool, pool_S[s - 1])
            pe_c[s] = pe.op(lambda e: e.matmul(ps[:, 2 + bi, c0:512], lhsT=ones, rhs=Sbf[1 - bi][:, c0:512], start=False, stop=True))

    def sb_back(s):
        nonlocal n_out
        qb, j, d, c0, first = info(s)
        bi = s % 2
        dve.wait(pe, pe_c[s])
        if s >= 2:
            dve.wait(act, act_A[s - 2])
        dve_arg[s] = dve.op(lambda e: e.tensor_tensor(out=argb[bi][:, c0:512], in0=ps[:, 2 + bi, c0:512], in1=spb[bi][:, c0:512], op=ALU.add), nosync=True)
        act.wait(dve, dve_arg[s])
        if s >= 2:
            act.wait(pe, pe_av[s - 2])
            if dve_mA[s - 2]:
                act.wait(dve, dve_mA[s - 2])
        act_A[s] = act.op(lambda e: e.activation(out=Ab[bi][:, c0:512], in_=argb[bi][:, c0:512], func=AF.Exp, scale=-1.0), nosync=True)
        if d >= 0:
            dve.wait(act, act_A[s])
            dve_mA[s] = dve.op(lambda e: e.tensor_tensor(out=Ab[bi][:, c0:c0 + 128], in0=Ab[bi][:, c0:c0 + 128], in1=maskLT, op=ALU.mult), nosync=True)
        pe.wait(act, act_A[s])
        if d >= 0:
            pe.wait(dve, dve_mA[s])
        if first and qb >= 1:
            for t in range(4):
                pe.wait(act, fin_sb[(qb - 1, t)])
        tl = list(range(max(0, d), 4))
        for t in tl:
            pe.op(lambda e, t=t: e.matmul(ps[:, 4 + t, 0:128], lhsT=Ab[bi][:, t * 128:(t + 1) * 128], rhs=Vs[:, j, :],
                                          start=(j == 4 * qb + t), stop=(j == 0)), count=(t == tl[-1]))
        pe_av[s] = pe.n
        if j == 0:
            for t in range(4):
                qt = 4 * qb + t
                ob = n_out % 2
                act.wait(pe, pe_av[s])
                if n_out >= 2:
                    act.wait(d_out[ob], 16 * (n_out // 2))
                fin_sb[(qb, t)] = act.op(lambda e, t=t, ob=ob: e.copy(out=oout[ob][:], in_=ps[:, 4 + t, 0:128]))
                sp.wait(act, fin_sb[(qb, t)])
                sp.dma(lambda e, ob=ob, qt=qt: e.dma_start(out=out_attn[qt * 128:(qt + 1) * 128, 128:256], in_=oout[ob][:]), d_out[ob])
                n_out += 1

    sb_qk(0)
    sb_front(0)
    for s in range(ns):
        if s + 1 < ns:
            sb_qk(s + 1)
        sb_cum(s)
        if s + 1 < ns:
            sb_front(s + 1)
        sb_back(s)
    sp.wait(d_out[0], d_out[0].n)
    sp.wait(d_out[1], d_out[1].n)
    return dict(pe=pe.n, act=act.n, dve=dve.n, pool=pool.n)


TT = 8
ALPHA = 2.0 ** 0.25
NE = 16384
LN_EPS = 1e-5
GELU = AF.Gelu


def build_phase_b(P, attn_in, wait_attn=None, NBLK=64, stop_after=None, gather=None):
    nc = P.nc
    pe, act, dve, pool, sp = P.pe, P.act, P.dve, P.pool, P.sp
    xs = P.dram("xs", [1024, D], F32, "ExternalInput")
    woutP = P.dram("woutP", [D, D], F32, "ExternalInput")
    lng = P.dram("lng", [3, D], F32, "ExternalInput")
    lnb = P.dram("lnb", [3, D], F32, "ExternalInput")
    wq = P.dram("wq", [D, D], F32, "ExternalInput")
    keysT = P.dram("keysT", [2, 128, 128], F32, "ExternalInput")
    uT = P.dram("uT", [D, NE], F32, "ExternalInput")
    vv = P.dram("vv", [NE, D], F32, "ExternalInput")
    wgate = P.dram("wgate", [D, D], F32, "ExternalInput")
    wproj = P.dram("wproj", [256, D], F32, "ExternalInput")
    pT = P.dram("pT", [256, 1024], F32, "ExternalInput")
    outB = P.dram("outB", [1024, D], F32, "ExternalOutput")

    X1 = P.sb("X1", [128, TT, D], F32)
    XT = P.sb("XT", [128, NCH, 1024], BF16)
    W0 = P.sb("W0", [128, 8192], BF16)
    W1 = P.sb("W1", [128, 8192], BF16)
    U = P.sb("U", [128, 8192], F32)
    identf = P.sb("identf", [128, 128], F32)
    identb = P.sb("identb", [128, 128], BF16)
    keys_sb = P.sb("keys_sb", [128, 2, 128], F32)
    st = P.sb("st", [128, 16], F32)
    gelT = [P.sb(f"gelT{i}", [128, 512], F32) for i in range(2)]
    qTc = gelT
    sE = [P.sb(f"sE{i}", [128, 8, 128], F32) for i in range(2)]
    gE = [P.sb(f"gE{i}", [128, 8, 128], BF16) for i in range(2)]
    diag = P.sb("diag", [128, 4, 8, 128], BF16)
    WT = [[P.sb(f"WT{a}{b}", [128, 512], BF16) for b in range(2)] for a in range(2)]
    swork = P.sb("swork", [128, 128], F32)
    ejunk = P.sb("ejunk", [128, 16], F32)
    cand = gelT[0][:, 0:484].rearrange("p (a b) -> p a b", a=22)
    cwork = gelT[1][:, 0:484]
    tk = P.sb("tk", [128, 80], F32)
    hst = P.sb("hst", [128, 4, 8, 8], F32)

    gb = U[:, 0:2048]
    bb = U[:, 2048:4096]
    attn_st = U[:, 4096:6144]
    sig = U[:, 4096:4608]
    tmpb = U[:, 4608:5120]
    junk = W0[:, 0:2048]
    pTf = U[:, 0:2048].rearrange("p (k t) -> p k t", k=2)
    projf = [U[:, 2048:4096], U[:, 5120:7168]]
    sbv = U[:, 0:4096].rearrange("p (t h n) -> p t h n", t=4, h=8)
    bav = U[:, 4096:8192].rearrange("p (t h n) -> p t h n", t=4, h=8)

    ps = P.ps("psB", [128, 8, 512], F32)

    d_xs = P.dsem("bxs")
    d_at = P.dsem("bat")
    d_ln = P.dsem("bln")
    d_w = [P.dsem("bw0"), P.dsem("bw1")]
    d_u = [P.dsem("bu0"), P.dsem("bu1")]
    d_v = [P.dsem("bv0"), P.dsem("bv1")]
    d_misc = P.dsem("bmisc")
    d_o = P.dsem("bout")

    W0v = W0[:].rearrange("p (c n) -> p c n", c=NCH)
    W1v = W1[:].rearrange("p (c n) -> p c n", c=NCH)
    Wv = [W0v, W1v]
    uTb = [W0[:, 0:4096].rearrange("p (c n) -> p c n", c=NCH), W0[:, 4096:8192].rearrange("p (c n) -> p c n", c=NCH)]
    vb = [W1[:, 0:4096].rearrange("p (k d) -> p k d", k=2), W1[:, 4096:8192].rearrange("p (k d) -> p k d", k=2)]

    pool.op(lambda e: e.memset(identf[:], 1.0))
    pool.op(lambda e: e.affine_select(out=identf[:], in_=identf[:], pattern=[[-1, 128]], compare_op=ALU.is_equal,
                                      fill=0.0, base=0, channel_multiplier=1))
    pool.op(lambda e: e.tensor_copy(out=identb[:], in_=identf[:]))
    ident_done = pool.n
    sp.dma(lambda e: e.dma_start(out=keys_sb[:], in_=keysT.ap().rearrange("s c n -> c s n")), d_misc)
    MISC = 16
    for h in range(4):
        sp.dma(lambda e, h=h: e.dma_start(out=X1[:, 2 * h:2 * h + 2, :],
                                          in_=xs.ap().rearrange("(t p) d -> p t d", p=128)[:, 2 * h:2 * h + 2, :]), d_xs)

    wstate = {"n": 0, "pe_done": {}}

    def load_wblock(src_ap_fn):
        n = wstate["n"]
        par = n % 2
        if n >= 2:
            pool.wait(pe, wstate["pe_done"][n - 2])
        for h in range(2):
            pool.dma(lambda e, h=h, par=par: e.dma_start(out=Wv[par][:, 8 * h:8 * h + 8, :], in_=src_ap_fn(h)), d_w[par])
        wstate["n"] += 1
        return n, par, d_w[par].n

    def wsrc(dram, nb):
        v = dram.ap().rearrange("(c p) n -> p c n", p=128)
        return lambda h: v[:, 8 * h:8 * h + 8, nb * 512:(nb + 1) * 512]

    def load_ln(k):
        sp.dma(lambda e: e.dma_start(out=gb, in_=lng.ap()[k:k + 1, :].partition_broadcast(128)), d_ln)
        sp.dma(lambda e: e.dma_start(out=bb, in_=lnb.ap()[k:k + 1, :].partition_broadcast(128)), d_ln)
        return d_ln.n

    def layer_norm(tt, lnval):
        y = X1[:, tt, :]
        dve.op(lambda e: e.reduce_sum(out=st[:, 0:1], in_=y, axis=AX.X))
        dve.op(lambda e: e.tensor_scalar(out=st[:, 1:2], in0=st[:, 0:1], scalar1=-1.0 / D, scalar2=None, op0=ALU.mult))
        i0 = dve.n
        act.wait(dve, i0)
        act.op(lambda e: e.activation(out=junk, in_=y, func=AF.Square, bias=st[:, 1:2], scale=1.0, accum_out=st[:, 2:3]))
        act.op(lambda e: e.activation(out=st[:, 3:4], in_=st[:, 2:3], func=AF.Ln, bias=LN_EPS, scale=1.0 / D))
        act.op(lambda e: e.activation(out=st[:, 4:5], in_=st[:, 3:4], func=AF.Exp, scale=-0.5))
        i1 = act.n
        dve.wait(act, i1)
        dve.wait(d_ln, lnval)
        dve.op(lambda e: e.tensor_scalar(out=y, in0=y, scalar1=st[:, 1:2], scalar2=st[:, 4:5], op0=ALU.add, op1=ALU.mult))
        dve.op(lambda e: e.tensor_tensor(out=y, in0=y, in1=gb, op=ALU.mult))
        return dve.op(lambda e: e.tensor_tensor(out=y, in0=y, in1=bb, op=ALU.add))

    tr_state = {"n": 0, "copy": [0, 0]}

    def transpose_tile(src_fn, tt, after):
        pe.wait(after[0], after[1])
        pe.wait(pool, ident_done)
        for g in range(4):
            n = tr_state["n"]
            bk = n % 2
            if n >= 2:
                pe.wait(act, tr_state["copy"][bk])
            for q in range(4):
                ch = 4 * g + q
                r = pe.op(lambda e, ch=ch, q=q, bk=bk: e.transpose(out=ps[:, bk, q * 128:(q + 1) * 128], in_=src_fn(ch), identity=identf[:]),
                          count=(q == 3))
            act.wait(pe, r)
            tr_state["copy"][bk] = act.op(lambda e, g=g, bk=bk: e.copy(out=XT[:, 4 * g:4 * g + 4, tt * 128:(tt + 1) * 128],
                                                                      in_=ps[:, bk, :].rearrange("p (q n) -> p q n", q=4)))
            tr_state["n"] += 1
        return tr_state["copy"][(tr_state["n"] - 1) % 2]

    ln1 = load_ln(0)
    if gather is None:
        if wait_attn is not None:
            sp.wait(wait_attn[0], wait_attn[1])
        for tt in range(TT):
            if tt >= 1:
                sp.wait(pe, tr_last_pe)
            sp.dma(lambda e, tt=tt: e.dma_start(out=attn_st.rearrange("p (s f) -> p s f", s=8),
                                                in_=attn_in.rearrange("s t f -> t s f")[tt * 128:(tt + 1) * 128, :, :]), d_at)
            last_copy = transpose_tile(lambda ch: attn_st[:, ch * 128:(ch + 1) * 128], tt, (d_at, d_at.n))
            tr_last_pe = pe.n
    else:
        I32 = mybir.dt.int32
        idx_sb = P.sb("idx_sb", [128, 64], I32)
        stage = [WT[q // 2][q % 2][:].rearrange("p (s f) -> p s f", s=2) for q in range(4)]
        d_idx = P.dsem("bidx")
        pool.dma(lambda e: e.dma_start(out=idx_sb[:], in_=gather), d_idx)
        pool.wait(d_idx, 16)
        pool.wait(wait_attn[0], wait_attn[1])
        cp = 0
        for tt in range(TT):
            if tt >= 1:
                pool.wait(dve, cp)
            for s_ in range(8):
                pool.dma(lambda e, tt=tt, s_=s_: e.indirect_dma_start(
                    out=stage[s_ // 2][:, s_ % 2, :], out_offset=None, in_=attn_in,
                    in_offset=bass.IndirectOffsetOnAxis(ap=idx_sb[:, tt * 8 + s_:tt * 8 + s_ + 1], axis=0)), d_at)
            dve.wait(d_at, d_at.n)
            if tt >= 1:
                dve.wait(pe, tr_last_pe)
            for q in range(4):
                cp = dve.op(lambda e, q=q: e.tensor_copy(out=attn_st[:, q * 512:(q + 1) * 512], in_=WT[q // 2][q % 2][:]))
            last_copy = transpose_tile(lambda ch: attn_st[:, ch * 128:(ch + 1) * 128], tt, (dve, cp))
            tr_last_pe = pe.n
    blocks = [load_wblock(wsrc(woutP, 0)), load_wblock(wsrc(woutP, 1))]
    pe.wait(act, last_copy)
    dve.wait(d_xs, 64)
    mm_bank = 0
    dve_mm = {}
    for nb in range(4):
        n, par, dval = blocks[nb]
        pe.wait(d_w[par], dval)
        for tt in range(TT):
            bk = 4 + (mm_bank % 4)
            if mm_bank >= 4:
                pe.wait(dve, dve_mm[mm_bank - 4])
            for ch in range(NCH):
                r = pe.op(lambda e, ch=ch, tt=tt, par=par, bk=bk: e.matmul(ps[:, bk, :], lhsT=XT[:, ch, tt * 128:(tt + 1) * 128], rhs=Wv[par][:, ch, :],
                                                                          start=(ch == 0), stop=(ch == NCH - 1)), count=(ch == NCH - 1))
            dve.wait(pe, r)
            dve_mm[mm_bank] = dve.op(lambda e, tt=tt, nb=nb, bk=bk: e.scalar_tensor_tensor(
                out=X1[:, tt, nb * 512:(nb + 1) * 512], in0=X1[:, tt, nb * 512:(nb + 1) * 512], scalar=ALPHA, in1=ps[:, bk, :],
                op0=ALU.mult, op1=ALU.add))
            mm_bank += 1
        wstate["pe_done"][n] = pe.n
        if nb + 2 < 4:
            blocks.append(load_wblock(wsrc(woutP, nb + 2)))
    ln_idx = []
    for tt in range(TT):
        i = layer_norm(tt, ln1)
        ln_idx.append(i)
        last_copy = transpose_tile(lambda ch, tt=tt: X1[:, tt, ch * 128:(ch + 1) * 128], tt, (dve, i))
    x1_done_dve = dve.n
    if stop_after == "ln1":
        return finish_out(P, X1, outB, d_o, (dve, x1_done_dve))

    dve.wait(pe, pe.n)
    for tt in range(TT):
        dve.op(lambda e, tt=tt: e.tensor_scalar(out=X1[:, tt, :], in0=X1[:, tt, :], scalar1=ALPHA, scalar2=None, op0=ALU.mult))
    peer_state = {"B": 0, "pe_blk": {}, "act_gel": {}, "dve_wt": {}, "dve_accb": [0, 0, 0, 0], "r": 0, "dve_g": {}, "pe_gt": {}, "act_e": {}, "pool_s": {}}
    uTv = uT.ap().rearrange("(c p) e -> p c e", p=128)
    vvv = vv.ap().rearrange("(b k p) d -> b p k d", k=2, p=128)

    def issue_peer_dma(B):
        eb = B % 64
        par = B % 2
        if B >= 2:
            pool.wait(pe, peer_state["pe_blk"][B - 2])
        for h in range(2):
            pool.dma(lambda e, h=h: e.dma_start(out=uTb[par][:, 8 * h:8 * h + 8, :], in_=uTv[:, 8 * h:8 * h + 8, eb * 256:(eb + 1) * 256]), d_u[par])
        pool.dma(lambda e: e.dma_start(out=vb[par], in_=vvv[eb]), d_v[par])
        return d_u[par].n, d_v[par].n

    def barrier():
        a, b, c = pe.n, act.n, dve.n
        pe.wait(act, b); pe.wait(dve, c)
        act.wait(pe, a); act.wait(dve, c)
        dve.wait(pe, a); dve.wait(act, b)

    for hf in range(2):
        tok0 = hf * 512
        barrier()
        wstate["n"] = 0
        wstate["pe_done"] = {}
        pool.wait(pe, pe.n)
        blocks = [load_wblock(wsrc(wq, 0)), load_wblock(wsrc(wq, 1))]
        pe.wait(act, last_copy)
        pe.wait(d_misc, MISC)
        qn = 0
        sc_copy = {}
        dve_q = {}
        act_sc = {}
        scn = 0
        for nb in range(4):
            n, par, dval = blocks[nb]
            pe.wait(d_w[par], dval)
            for c4 in range(4):
                ct = nb * 4 + c4
                h, side = ct // 2, ct % 2
                qb = qn % 2
                if qn >= 2:
                    pe.wait(dve, dve_q[qn - 2])
                for ch in range(NCH):
                    r = pe.op(lambda e, ch=ch, c4=c4, par=par, qb=qb: e.matmul(ps[:, qb, :], lhsT=Wv[par][:, ch, c4 * 128:(c4 + 1) * 128],
                                                                              rhs=XT[:, ch, tok0:tok0 + 512], start=(ch == 0), stop=(ch == NCH - 1)),
                              count=(ch == NCH - 1))
                dve.wait(pe, r)
                if qn >= 2:
                    dve.wait(pe, sc_pe[qn - 2])
                dve_q[qn] = dve.op(lambda e, qb=qb: e.tensor_copy(out=qTc[qb][:], in_=ps[:, qb, :]))
                pe.wait(dve, dve_q[qn])
                if qn >= 2:
                    pe.wait(act, act_sc[qn - 2])
                for tt in range(4):
                    r = pe.op(lambda e, tt=tt, qb=qb, side=side: e.matmul(ps[:, 2 + qb, tt * 128:(tt + 1) * 128], lhsT=qTc[qb][:, tt * 128:(tt + 1) * 128],
                                                                         rhs=keys_sb[:, side, :], start=True, stop=True), count=(tt == 3))
                if qn == 0:
                    sc_pe = {}
                sc_pe[qn] = r
                act.wait(pe, r)
                if side == 0:
                    act_sc[qn] = act.op(lambda e, h=h, qb=qb: e.copy(out=bav[:, :, h, :], in_=ps[:, 2 + qb, :].rearrange("p (t n) -> p t n", t=4)))
                else:
                    act_sc[qn] = act.op(lambda e, h=h, qb=qb: e.copy(out=sbv[:, :, h, :], in_=ps[:, 2 + qb, :].rearrange("p (t n) -> p t n", t=4)))
                qn += 1
            wstate["pe_done"][n] = pe.n
            if nb + 2 < 4:
                blocks.append(load_wblock(wsrc(wq, nb + 2)))
        scores_done_act = act.n
        scores_done_pe = pe.n
        B0 = peer_state["B"]
        pool.wait(pe, scores_done_pe)
        dma_vals = {}
        dma_vals[B0] = issue_peer_dma(B0) if NBLK > 0 else None

        dve.wait(act, scores_done_act)
        dve.wait(pool, ident_done)
        for tt in range(4):
            for h in range(8):
                sa = bav[:, tt, h, :]
                sbb_ = sbv[:, tt, h, :]
                for (src, off) in [(sa, 0), (sbb_, 24)]:
                    dve.op(lambda e, src=src, off=off: e.max(out=tk[:, off:off + 8], in_=src))
                    dve.op(lambda e, src=src, off=off: e.match_replace(out=swork[:], in_to_replace=tk[:, off:off + 8], in_values=src, imm_value=-1e30))
                    dve.op(lambda e, off=off: e.max(out=tk[:, off + 8:off + 16], in_=swork[:]))
                    dve.op(lambda e, off=off: e.match_replace(out=swork[:], in_to_replace=tk[:, off + 8:off + 16], in_values=swork[:], imm_value=-1e30))
                    dve.op(lambda e, off=off: e.max(out=tk[:, off + 16:off + 24], in_=swork[:]))
                dve.op(lambda e: e.tensor_tensor(out=cand, in0=tk[:, 0:22].unsqueeze(2).broadcast_to([128, 22, 22]),
                                                 in1=tk[:, 24:46].unsqueeze(1).broadcast_to([128, 22, 22]), op=ALU.add))
                cf = gelT[0][:, 0:484]
                dve.op(lambda e: e.max(out=tk[:, 48:56], in_=cf))
                dve.op(lambda e: e.match_replace(out=cwork, in_to_replace=tk[:, 48:56], in_values=cf, imm_value=-1e30))
                dve.op(lambda e: e.max(out=tk[:, 56:64], in_=cwork))
                dve.op(lambda e: e.match_replace(out=cwork, in_to_replace=tk[:, 56:64], in_values=cwork, imm_value=-1e30))
                dve.op(lambda e: e.max(out=tk[:, 64:72], in_=cwork))
                hs = hst[:, tt, h, :]
                dve.op(lambda e, hs=hs: e.tensor_scalar(out=hs[:, 0:1], in0=tk[:, 48:49], scalar1=-1.0, scalar2=None, op0=ALU.mult))
                i0 = dve.n
                act.wait(dve, i0)
                act.op(lambda e, hs=hs: e.activation(out=ejunk[:], in_=tk[:, 48:64], func=AF.Exp, bias=hs[:, 0:1], scale=1.0,
                                                     accum_out=hs[:, 1:2]))
                act.op(lambda e, hs=hs: e.activation(out=hs[:, 2:3], in_=hs[:, 1:2], func=AF.Ln))
                i1 = act.n
                dve.wait(act, i1)
                dve.op(lambda e, hs=hs: e.tensor_tensor(out=hs[:, 4:5], in0=hs[:, 0:1], in1=hs[:, 2:3], op=ALU.subtract))
                dve.op(lambda e, hs=hs: e.tensor_tensor(out=hs[:, 5:6], in0=tk[:, 63:64], in1=tk[:, 64:65], op=ALU.add))
                dve.op(lambda e, hs=hs: e.tensor_scalar(out=hs[:, 7:8], in0=hs[:, 5:6], scalar1=-0.5, scalar2=None, op0=ALU.mult))
                dve.op(lambda e, sa=sa, hs=hs: e.tensor_scalar(out=sa, in0=sa, scalar1=hs[:, 7:8], scalar2=None, op0=ALU.add))
                i2 = dve.n
                act.wait(dve, i2)
                act.op(lambda e, hs=hs: e.activation(out=hs[:, 6:7], in_=hs[:, 5:6], func=AF.Exp, bias=hs[:, 4:5], scale=0.5))
                dve.wait(act, act.n)
                dve.op(lambda e, tt=tt, h=h, hs=hs: e.tensor_scalar(out=diag[:, tt, h, :], in0=identf[:], scalar1=hs[:, 6:7], scalar2=None, op0=ALU.mult))
        topk_done_dve = dve.n
        topk_done_act = act.n
        if stop_after == "topk":
            dbgU = P.dram("dbgU", [128, 8192], F32, "ExternalOutput")
            dbgH = P.dram("dbgH", [128, 256], F32, "ExternalOutput")
            sp.wait(dve, topk_done_dve)
            sp.wait(act, topk_done_act)
            sp.dma(lambda e: e.dma_start(out=dbgU.ap(), in_=U[:]), d_o)
            sp.dma(lambda e: e.dma_start(out=dbgH.ap(), in_=hst[:].rearrange("p a b c -> p (a b c)")), d_o)
            return finish_out(P, X1, outB, d_o, (dve, topk_done_dve))

        pool.wait(dve, topk_done_dve)
        pool.wait(act, topk_done_act)
        for ebi in range(NBLK):
            B = peer_state["B"]
            par = B % 2
            uval, vval = dma_vals[B]
            if ebi + 1 < NBLK:
                dma_vals[B + 1] = issue_peer_dma(B + 1)
            pe.wait(d_u[par], uval)
            pe.wait(d_v[par], vval)
            pe_ht = {}
            for k in range(2):
                if B >= 1:
                    pe.wait(act, peer_state["act_gel"][(B - 1, k)])
                for ch in range(NCH):
                    r = pe.op(lambda e, ch=ch, k=k: e.matmul(ps[:, k, :], lhsT=uTb[par][:, ch, k * 128:(k + 1) * 128], rhs=XT[:, ch, tok0:tok0 + 512],
                                                             start=(ch == 0), stop=(ch == NCH - 1)), count=(ch == NCH - 1))
                pe_ht[k] = r
            for k in range(2):
                act.wait(pe, pe_ht[k])
                if B >= 1:
                    act.wait(dve, peer_state["dve_wt"][(B - 1, k)])
                peer_state["act_gel"][(B, k)] = act.op(lambda e, k=k: e.activation(out=gelT[k][:], in_=ps[:, k, :], func=GELU))
            for k in range(2):
                i = 2 * (B % 64) + k
                if B >= 1:
                    pe.wait(dve, peer_state["dve_wt"][(B - 1, k)])
                for tt in range(4):
                    r = peer_state["r"]
                    rb = r % 2
                    if r >= 2:
                        pool.wait(dve, peer_state["dve_g"][r - 2])
                    peer_state["pool_s"][r] = pool.op(lambda e, rb=rb, tt=tt, i=i: e.tensor_tensor(
                        out=sE[rb][:], in0=sbv[:, tt, :, :], in1=bav[:, tt, :, i:i + 1].broadcast_to([128, 8, 128]), op=ALU.add), nosync=True)
                    act.wait(pool, peer_state["pool_s"][r])
                    peer_state["act_e"][r] = act.op(lambda e, rb=rb: e.activation(out=sE[rb][:], in_=sE[rb][:], func=AF.Exp), nosync=True)
                    dve.wait(act, peer_state["act_e"][r])
                    if r >= 2:
                        dve.wait(pe, peer_state["pe_gt"][r - 2])
                    peer_state["dve_g"][r] = dve.op(lambda e, rb=rb: e.scalar_tensor_tensor(
                        out=gE[rb][:], in0=sE[rb][:], scalar=1.0, in1=sE[rb][:], op0=ALU.is_ge, op1=ALU.mult), nosync=True)
                    pe.wait(dve, peer_state["dve_g"][r])
                    for h in range(8):
                        rr = pe.op(lambda e, rb=rb, tt=tt, h=h, k=k: e.matmul(
                            ps[:, 2 + k, tt * 128:(tt + 1) * 128], lhsT=gE[rb][:, h, :], rhs=diag[:, tt, h, :], start=(h == 0), stop=(h == 7)),
                            count=(h == 7))
                    peer_state["pe_gt"][r] = rr
                    peer_state["r"] += 1
                gt_done = pe.n
                dve.wait(pe, gt_done)
                dve.wait(act, peer_state["act_gel"][(B, k)])
                if B >= 2:
                    dve.wait(pe, peer_state["pe_blk"][B - 2])
                peer_state["dve_wt"][(B, k)] = dve.op(lambda e, k=k: e.tensor_tensor(out=WT[par][k][:], in0=ps[:, 2 + k, :], in1=gelT[k][:], op=ALU.mult))
            pe.wait(dve, peer_state["dve_wt"][(B, 0)])
            pe.wait(dve, peer_state["dve_wt"][(B, 1)])
            for tt in range(4):
                gtt = hf * 4 + tt
                for nb in range(4):
                    pe.wait(dve, peer_state["dve_accb"][nb])
                    for k in range(2):
                        r = pe.op(lambda e, tt=tt, nb=nb, k=k: e.matmul(ps[:, 4 + nb, :], lhsT=WT[par][k][:, tt * 128:(tt + 1) * 128],
                                                                      rhs=vb[par][:, k, nb * 512:(nb + 1) * 512], start=(k == 0), stop=(k == 1)),
                                  count=(k == 1))
                    dve.wait(pe, r)
                    peer_state["dve_accb"][nb] = dve.op(lambda e, gtt=gtt, nb=nb: e.tensor_tensor(
                        out=X1[:, gtt, nb * 512:(nb + 1) * 512], in0=X1[:, gtt, nb * 512:(nb + 1) * 512], in1=ps[:, 4 + nb, :], op=ALU.add), nosync=True)
            peer_state["pe_blk"][B] = pe.n
            peer_state["B"] += 1
        act.wait(dve, dve.n)
        act.wait(pe, pe.n)
    peer_done_dve = dve.n
    if stop_after == "dbgwt":
        dbgW = P.dram("dbgW", [2, 128, 512], BF16, "ExternalOutput")
        dbgG = P.dram("dbgG", [2, 128, 512], F32, "ExternalOutput")
        sp.wait(dve, peer_done_dve)
        sp.wait(act, act.n)
        for k in range(2):
            sp.dma(lambda e, k=k: e.dma_start(out=dbgW.ap()[k], in_=WT[0][k][:]), d_o)
            sp.dma(lambda e, k=k: e.dma_start(out=dbgG.ap()[k], in_=gelT[k][:]), d_o)
        return finish_out(P, X1, outB, d_o, (dve, peer_done_dve))
    if stop_after == "peer":
        return finish_out(P, X1, outB, d_o, (dve, peer_done_dve))

    sp.wait(dve, peer_done_dve)
    sp.wait(act, act.n)
    ln2 = load_ln(1)
    pe.wait(dve, peer_done_dve)
    for tt in range(TT):
        i = layer_norm(tt, ln2)
        last_copy = transpose_tile(lambda ch, tt=tt: X1[:, tt, ch * 128:(ch + 1) * 128], tt, (dve, i))
    ln2_done = dve.n
    d_pp = P.dsem("bpp")
    sp.wait(dve, ln2_done)
    sp.dma(lambda e: e.dma_start(out=pTf, in_=pT.ap().rearrange("(k p) t -> p k t", p=128)), d_pp)
    for k in range(2):
        sp.dma(lambda e, k=k: e.dma_start(out=projf[k], in_=wproj.ap()[k * 128:(k + 1) * 128, :]), d_pp)
    wstate["n"] = 0
    wstate["pe_done"] = {}
    pool.wait(pe, pe.n)
    blocks = [load_wblock(wsrc(wgate, 0)), load_wblock(wsrc(wgate, 1))]
    pe.wait(act, last_copy)
    pe.wait(d_pp, 48)
    cnt = 0
    dve_y = {}
    act_sig = {}
    for nb in range(4):
        n, par, dval = blocks[nb]
        pe.wait(d_w[par], dval)
        for tt in range(TT):
            b0 = (cnt % 2) * 2
            if cnt >= 2:
                pe.wait(dve, dve_y[cnt - 2])
            for ch in range(NCH):
                pe.op(lambda e, ch=ch, tt=tt, par=par, b0=b0: e.matmul(ps[:, b0, :], lhsT=XT[:, ch, tt * 128:(tt + 1) * 128], rhs=Wv[par][:, ch, :],
                                                                      start=(ch == 0), stop=(ch == NCH - 1)), count=False)
            for k in range(2):
                r = pe.op(lambda e, k=k, tt=tt, nb=nb, b0=b0: e.matmul(ps[:, b0 + 1, :], lhsT=pTf[:, k, tt * 128:(tt + 1) * 128],
                                                                      rhs=projf[k][:, nb * 512:(nb + 1) * 512], start=(k == 0), stop=(k == 1)), count=(k == 1))
            act.wait(pe, r)
            if cnt >= 1:
                act.wait(dve, dve_y[cnt - 1])
            act_sig[cnt] = act.op(lambda e, b0=b0: e.activation(out=sig, in_=ps[:, b0, :], func=AF.Sigmoid))
            dve.wait(act, act_sig[cnt])
            dve.op(lambda e, b0=b0: e.tensor_tensor(out=tmpb, in0=sig, in1=ps[:, b0 + 1, :], op=ALU.mult))
            dve_y[cnt] = dve.op(lambda e, tt=tt, nb=nb: e.scalar_tensor_tensor(out=X1[:, tt, nb * 512:(nb + 1) * 512], in0=X1[:, tt, nb * 512:(nb + 1) * 512],
                                                                              scalar=ALPHA, in1=tmpb, op0=ALU.mult, op1=ALU.add))
            cnt += 1
        wstate["pe_done"][n] = pe.n
        if nb + 2 < 4:
            blocks.append(load_wblock(wsrc(wgate, nb + 2)))
    ple_done = dve.n
    sp.wait(dve, ple_done)
    ln3 = load_ln(2)
    for tt in range(TT):
        i = layer_norm(tt, ln3)
        sp.wait(dve, i)
        sp.dma(lambda e, tt=tt: e.dma_start(out=outB.ap()[tt * 128:(tt + 1) * 128, :], in_=X1[:, tt, :]), d_o)
    sp.wait(d_o, d_o.n)
    return None


def finish_out(P, X1, outB, d_o, after):
    sp = P.sp
    sp.wait(after[0], after[1])
    for tt in range(TT):
        sp.dma(lambda e, tt=tt: e.dma_start(out=outB.ap()[tt * 128:(tt + 1) * 128, :], in_=X1[:, tt, :]), d_o)
    sp.wait(d_o, d_o.n)
    return None


I32 = mybir.dt.int32
_CACHE = {}


def build_fused():
    P = Prog()
    main_stack = P.stack
    attn_loc_t = P.dram("attn_loc", [512, 4096], BF16, "Internal")
    attn_all_t = P.dram("attn_all", [4096, 4096], BF16, "Internal")
    attn_loc_v = attn_loc_t.ap().rearrange("a (b f) -> (a b) f", f=256)
    attn_all_v = attn_all_t.ap().rearrange("a (b f) -> (a b) f", f=256)
    aidx = P.dram("aidx", [128, 64], I32, "ExternalInput")
    cc = P.dsem("cc")
    P.stack = contextlib.ExitStack()
    build_phase_a(P, attn_loc_v)
    P.stack.close()
    for d in P.dsems:
        if d.name in ("o0", "o1"):
            P.pool.wait(d, d.n)
    rg = [list(range(8))]
    P.pool.cc(lambda e: e.collective_compute("AllGather", ALU.bypass, replica_groups=rg,
                                             ins=[attn_loc_t.ap()], outs=[attn_all_t.ap()]), cc)
    P.stack = contextlib.ExitStack()
    build_phase_b(P, attn_all_v, wait_attn=(cc, 1), gather=aidx.ap())
    sub = P.stack
    P.stack = main_stack
    nc = P.finish()
    sub.close()
    return nc


def _rope_tables():
    pos = np.arange(S, dtype=np.float32)
    inv = (np.float32(500000.0) ** (-np.arange(0, 16, 2, dtype=np.float32) / np.float32(16))).astype(np.float32)
    ang = (pos[:, None] * inv[None, :]).astype(np.float32)
    cos = np.cos(ang).astype(np.float32).T
    sin = np.sin(ang).astype(np.float32).T
    C = np.ones((128, S), np.float32)
    Sn = np.zeros((128, S), np.float32)
    for base in (0, 64):
        C[base:base + 8] = cos
        C[base + 8:base + 16] = cos
        Sn[base:base + 8] = -sin
        Sn[base + 8:base + 16] = sin
    return C, Sn


def host_prep(inp):
    x = np.asarray(inp["x"], np.float32)[0]
    w_in = np.asarray(inp["w_in"], np.float32)[0]
    w_out = np.asarray(inp["w_out"], np.float32)[0]
    xT = np.ascontiguousarray(x.T)
    C, Sn = _rope_tables()
    perm = np.arange(128)
    for base in (0, 64):
        perm[base:base + 8] = np.arange(base + 8, base + 16)
        perm[base + 8:base + 16] = np.arange(base, base + 8)
    lamv = np.concatenate([np.asarray(inp[k], np.float32)[0] for k in ("lambda_q1", "lambda_k1", "lambda_q2", "lambda_k2")])[None, :]
    lamv = np.ascontiguousarray(lamv, dtype=np.float32)
    gsub = np.asarray(inp["diff_subln_g"], np.float32).reshape(1, 128)
    rows = []
    for s in range(8):
        rows.append(w_out[s * 128:(s + 1) * 128])
        rows.append(w_out[1024 + s * 128:1024 + (s + 1) * 128])
    woutP = np.ascontiguousarray(np.concatenate(rows, axis=0))
    lng = np.ascontiguousarray(np.asarray(inp["ln_g"], np.float32)[0])
    lnb = np.ascontiguousarray(np.asarray(inp["ln_b"], np.float32)[0])
    wq = np.ascontiguousarray(np.asarray(inp["peer_wq"], np.float32)[0])
    keysT = np.ascontiguousarray(np.transpose(np.asarray(inp["peer_keys"], np.float32)[0], (0, 2, 1)))
    uT = np.ascontiguousarray(np.asarray(inp["peer_u"], np.float32)[0].T)
    vv = np.ascontiguousarray(np.asarray(inp["peer_v"], np.float32)[0])
    wgate = np.ascontiguousarray(np.asarray(inp["ple_gate"], np.float32)[0])
    wproj = np.ascontiguousarray(np.asarray(inp["ple_proj"], np.float32)[0])
    p = np.asarray(inp["p"], np.float32)[0, 0]
    maps = []
    for c in range(8):
        sl = lambda off: w_in[:, off + c * 128: off + (c + 1) * 128]
        dq, dk, dv, sq, sk, sv = sl(0), sl(1024), sl(2048), sl(3072), sl(4096), sl(5120)
        wA = np.ascontiguousarray(np.concatenate([dq, dk, sq, sk, dq[:, perm], dk[:, perm], dv, sv], axis=1))
        tok = slice(c * 1024, (c + 1) * 1024)
        aidx = np.zeros((128, 64), np.int32)
        for tt in range(8):
            for s in range(8):
                aidx[:, tt * 8 + s] = s * S + c * 1024 + tt * 128 + np.arange(128)
        maps.append({"xT": xT, "wA": wA, "ropeC": C, "ropeS": Sn, "lamv": lamv, "gsub": gsub,
                     "aidx": aidx, "xs": np.ascontiguousarray(x[tok]), "woutP": woutP, "lng": lng, "lnb": lnb,
                     "wq": wq, "keysT": keysT, "uT": uT, "vv": vv, "wgate": wgate, "wproj": wproj,
                     "pT": np.ascontiguousarray(p[tok].T)})
    return maps


def kernel(**inputs):
    if "nc" not in _CACHE:
        _CACHE["nc"] = build_fused()
    nc = _CACHE["nc"]
    maps = host_prep(inputs)
    res = run_bass_kernel_spmd(nc, maps, core_ids=list(range(8)))
    out = np.concatenate([np.asarray(res.results[c]["outB"], dtype=np.float32) for c in range(8)], axis=0)
    return out.reshape(1, S, D)
```
